# Optimizing a Trainium2 kernel written in Bass

```python
import math
import jax, jax.numpy as jnp
from jax import lax
import numpy as np

D_MODEL = 2048
BATCH = 4
SEQ = 2048
DEPTH = 1
DEC_BATCH = 128
DEC_SEQ = 4
PAST_LEN = 16384
PAGE_SIZE = 128

MIX_WIDTH = D_MODEL
C_CONV = MIX_WIDTH // 2
C_POOL = MIX_WIDTH - C_CONV
CONV_HEADS = 8
CONV_HEAD_DIM = C_CONV // CONV_HEADS
CONV_WIDTH = 31
POOL_WINDOWS = (2, 4, 8, 16)
N_POOL_GROUPS = len(POOL_WINDOWS)
POOL_GROUP = C_POOL // N_POOL_GROUPS
MAX_WINDOW = max(POOL_WINDOWS)
D_FF = 4 * D_MODEL
N_META = 16
EPS = 1e-6

kernel_name = "hymba_conformer_conv_multiscale_pool_decoder_step"


def rmsnorm(x, g):
    xf = x.astype(jnp.float32)
    y = xf * lax.rsqrt(jnp.mean(xf * xf, axis=-1, keepdims=True) + EPS)
    return (y * g.astype(jnp.float32)).astype(x.dtype)


def head_layernorm(u, g, b):
    n, t, c = u.shape
    uf = u.astype(jnp.float32).reshape(n, t, CONV_HEADS, CONV_HEAD_DIM)
    mu = jnp.mean(uf, axis=-1, keepdims=True)
    var = jnp.mean(jnp.square(uf - mu), axis=-1, keepdims=True)
    y = ((uf - mu) * lax.rsqrt(var + EPS)).reshape(n, t, c)
    return (y * g.astype(jnp.float32) + b.astype(jnp.float32)).astype(u.dtype)


def depthwise_causal_conv(u_full, w_dw, b_dw):
    c = u_full.shape[-1]
    y = lax.conv_general_dilated(
        u_full, w_dw[:, None, :].astype(u_full.dtype), window_strides=(1,), padding='VALID',
        dimension_numbers=('NWC', 'WIO', 'NWC'), feature_group_count=c)
    return y + b_dw.astype(y.dtype)


def multiscale_pool(hist, xb, pos0, w_pool, pool_scale):
    L = hist.shape[1]
    T = xb.shape[1]
    z = jnp.concatenate([hist, xb], axis=1).astype(jnp.float32)
    csum = jnp.concatenate([jnp.zeros_like(z[:, :1]), jnp.cumsum(z, axis=1)], axis=1)
    pos = jnp.arange(T, dtype=jnp.float32) + pos0
    xf = xb.astype(jnp.float32)
    outs = []
    for gi, w in enumerate(POOL_WINDOWS):
        sl = slice(gi * POOL_GROUP, (gi + 1) * POOL_GROUP)
        s = csum[:, L + 1:L + 1 + T, sl] - csum[:, L + 1 - w:L + 1 - w + T, sl]
        cnt = jnp.minimum(jnp.float32(w), pos + 1.0)
        d = s / cnt[None, :, None] - xf[..., sl]
        outs.append(jnp.einsum('btc,cd->btd', d, w_pool[gi].astype(jnp.float32)))
    y = jnp.concatenate(outs, axis=-1) * pool_scale.astype(jnp.float32)
    return y.astype(xb.dtype)


def mixer(h, conv_hist, pool_hist, pos0, w_in, w_dw, b_dw, ln_g, ln_b, w_pool, pool_scale, w_out):
    proj = jnp.einsum('btd,de->bte', h, w_in)
    a = proj[..., :C_CONV]
    gate = proj[..., C_CONV:2 * C_CONV]
    xb = proj[..., 2 * C_CONV:]
    u = a * jax.nn.sigmoid(gate)
    u_full = jnp.concatenate([conv_hist.astype(u.dtype), u], axis=1)
    c = depthwise_causal_conv(u_full, w_dw, b_dw)
    c = jax.nn.silu(head_layernorm(c, ln_g, ln_b))
    p = multiscale_pool(pool_hist.astype(xb.dtype), xb, pos0, w_pool, pool_scale)
    y = jnp.einsum('bte,ed->btd', jnp.concatenate([c, p], axis=-1), w_out)
    new_conv = u_full[:, -(CONV_WIDTH - 1):]
    new_pool = jnp.concatenate([pool_hist.astype(xb.dtype), xb], axis=1)[:, -(MAX_WINDOW - 1):]
    return y, new_conv, new_pool


def sq_relu_mlp(h, w_up, w_down):
    a = jax.nn.relu(jnp.einsum('btd,df->btf', h, w_up))
    return jnp.einsum('btf,fd->btd', a * a, w_down)


def setup_inputs(seed: int = 0) -> dict:
    key = jax.random.key(seed)
    ks = jax.random.split(key, 20)
    f32 = jnp.float32
    n = lambda k, s, sc: jax.random.normal(k, s, f32) * sc
    return {
        "x_prompt": n(ks[0], (BATCH, SEQ, D_MODEL), 1.0),
        "x_sample": n(ks[1], (DEC_BATCH, DEC_SEQ, D_MODEL), 1.0),
        "state_conv": n(ks[2], (DEPTH, DEC_BATCH, CONV_WIDTH - 1, C_CONV), 0.5),
        "state_pool": n(ks[3], (DEPTH, DEC_BATCH, MAX_WINDOW - 1, C_POOL), 1.0),
        "meta_tokens": n(ks[4], (N_META, D_MODEL), 1.0),
        "norm_mix_g": 1.0 + n(ks[5], (DEPTH, D_MODEL), 0.02),
        "w_in": n(ks[6], (DEPTH, D_MODEL, 2 * C_CONV + C_POOL), D_MODEL ** -0.5),
        "w_dw": n(ks[7], (DEPTH, CONV_WIDTH, C_CONV), CONV_WIDTH ** -0.5),
        "b_dw": n(ks[8], (DEPTH, C_CONV), 0.02),
        "conv_ln_g": 1.0 + n(ks[9], (DEPTH, C_CONV), 0.02),
        "conv_ln_b": n(ks[10], (DEPTH, C_CONV), 0.02),
        "w_pool": n(ks[11], (DEPTH, N_POOL_GROUPS, POOL_GROUP, POOL_GROUP), POOL_GROUP ** -0.5),
        "pool_scale": 1.0 + n(ks[12], (DEPTH, C_POOL), 0.1),
        "w_out": n(ks[13], (DEPTH, C_CONV + C_POOL, D_MODEL), (C_CONV + C_POOL) ** -0.5),
        "norm_ffn_g": 1.0 + n(ks[14], (DEPTH, D_MODEL), 0.02),
        "w_up": n(ks[15], (DEPTH, D_MODEL, D_FF), D_MODEL ** -0.5),
        "w_down": n(ks[16], (DEPTH, D_FF, D_MODEL), D_FF ** -0.5),
        "final_norm_g": 1.0 + n(ks[17], (D_MODEL,), 0.02),
    }


def reference(x_prompt, x_sample, state_conv, state_pool, meta_tokens, norm_mix_g, w_in, w_dw, b_dw,
              conv_ln_g, conv_ln_b, w_pool, pool_scale, w_out, norm_ffn_g, w_up, w_down, final_norm_g):
    meta = jnp.broadcast_to(meta_tokens.astype(x_prompt.dtype)[None], (x_prompt.shape[0], N_META, D_MODEL))
    xp = jnp.concatenate([meta, x_prompt], axis=1)
    xs = x_sample
    nb = xp.shape[0]
    zero_conv = jnp.zeros((nb, CONV_WIDTH - 1, C_CONV), xp.dtype)
    zero_pool = jnp.zeros((nb, MAX_WINDOW - 1, C_POOL), xp.dtype)
    conv_p, pool_p, conv_s, pool_s = [], [], [], []
    for l in range(DEPTH):
        wl = (w_in[l], w_dw[l], b_dw[l], conv_ln_g[l], conv_ln_b[l], w_pool[l], pool_scale[l], w_out[l])
        yp, cp, pp = mixer(rmsnorm(xp, norm_mix_g[l]), zero_conv, zero_pool, 0, *wl)
        xp = xp + yp
        xp = xp + sq_relu_mlp(rmsnorm(xp, norm_ffn_g[l]), w_up[l], w_down[l])
        ys, cs, ps = mixer(rmsnorm(xs, norm_mix_g[l]), state_conv[l], state_pool[l], PAST_LEN, *wl)
        xs = xs + ys
        xs = xs + sq_relu_mlp(rmsnorm(xs, norm_ffn_g[l]), w_up[l], w_down[l])
        conv_p.append(cp); pool_p.append(pp); conv_s.append(cs); pool_s.append(ps)
    y_prompt = rmsnorm(xp, final_norm_g)[:, N_META:]
    y_sample = rmsnorm(xs, final_norm_g)
    new_conv_prompt = jnp.stack(conv_p, axis=0)
    new_pool_prompt = jnp.stack(pool_p, axis=0)
    new_conv_sample = jnp.stack(conv_s, axis=0)
    new_pool_sample = jnp.stack(pool_s, axis=0)
    return (y_prompt, y_sample, new_conv_prompt, new_pool_prompt, new_conv_sample, new_pool_sample)
```

```python
import contextlib
import numpy as np
import concourse.bass as bass
import concourse.mybir as mybir
from concourse.bass_utils import run_bass_kernel_spmd

F32 = mybir.dt.float32
BF16 = mybir.dt.bfloat16
AF = mybir.ActivationFunctionType
ALU = mybir.AluOpType

D = 2048
NPR = 1024
NH = 30
NS = 64
NSEQ = 16
PC = NH + NPR
NCOL = PC + NS
NTOK = NPR + NS
CC = 1024
DFF = 8192
EPS = 1e-6
WINDOWS = (2, 4, 8, 16)
ARENA_BYTES = 212000


class Eng:
    def __init__(self, nc, stack, handle, name, self_wait=True):
        self.h = handle
        self.key = name
        self.sem = stack.enter_context(nc.semaphore("sem_" + name))
        self.n = 0
        self.seen = {}
        self.self_wait = self_wait

    def wait(self, *toks):
        for t in toks:
            if t is None:
                continue
            key, sem, val = t
            if key == self.key and not self.self_wait:
                continue
            if self.seen.get(key, 0) >= val:
                continue
            self.h.wait_ge(sem, val)
            self.seen[key] = val

    def done(self, ins):
        ins.then_inc(self.sem, 1)
        self.n += 1
        return (self.key, self.sem, self.n)


class DmaSem:
    def __init__(self, nc, stack, name):
        self.key = "dma_" + name
        self.sem = stack.enter_context(nc.semaphore("dsem_" + name))
        self.n = 0

    def done(self, ins):
        ins.then_inc(self.sem, 16)
        self.n += 16
        return (self.key, self.sem, self.n)


class Res:
    def __init__(self):
        self.w = None
        self.r = {}


def _pre(eng, reads, writes, free=()):
    for r in reads:
        eng.wait(r.w)
    for r in list(writes) + list(free):
        eng.wait(r.w)
        eng.wait(*r.r.values())


def _post(tok, reads, writes):
    for r in reads:
        r.r[tok[0]] = tok
    for r in writes:
        r.w = tok
        r.r = {}


def op(eng, fn, reads=(), writes=(), free=()):
    _pre(eng, reads, writes, free)
    ins = fn()
    tok = eng.done(ins)
    _post(tok, reads, writes)
    return tok


def dma(queue, dsem, out, in_, reads=(), writes=(), free=()):
    _pre(queue, reads, writes, free)
    ins = queue.h.dma_start(out=out, in_=in_)
    tok = dsem.done(ins)
    _post(tok, reads, writes)
    return tok


def bc(ap, dims):
    return bass.AP(ap.tensor, ap.offset, [list(ap.ap[0])] + [list(d) for d in dims])


def build(debug=False):
    nc = bass.Bass("TRN2", target_bir_lowering=False)
    dr = lambda n, s, k="ExternalInput": nc.dram_tensor(n, s, F32, kind=k)
    xin_t = dr("xin", [NCOL + 2, D])
    sc_t = dr("sc", [NSEQ, 30, CC])
    sp_t = dr("sp", [NSEQ, 15, CC])
    gs_t = dr("gs", [3, D])
    prm_t = dr("prm", [35, CC])
    win_t = dr("w_in", [D, 3 * CC])
    wpool_t = dr("w_pool", [4, 256, 256])
    wout_t = dr("w_out", [D, D])
    wup_t = dr("w_up", [D, DFF])
    wdn_t = dr("w_down", [DFF, D])
    y_t = dr("y", [NTOK, D], "ExternalOutput")
    ncs_t = dr("ncs", [NSEQ, 30, CC], "ExternalOutput")
    nps_t = dr("nps", [NSEQ, 15, CC], "ExternalOutput")
    ncp_t = dr("ncp", [30, CC], "ExternalOutput")
    npp_t = dr("npp", [15, CC], "ExternalOutput")
    if debug:
        dbg_ct = nc.dram_tensor("dbg_ct", [128, 8 * NTOK], BF16, kind="ExternalOutput").ap()
        dbg_dt = nc.dram_tensor("dbg_dt", [128, 8 * NTOK], BF16, kind="ExternalOutput").ap()
        dbg_x1 = nc.dram_tensor("dbg_x1", [9, 128, D], F32, kind="ExternalOutput").ap()
    xin, sc, sp, prm_d = xin_t.ap(), sc_t.ap(), sp_t.ap(), prm_t.ap()
    w_in, w_pool, w_out, w_up, w_dn = win_t.ap(), wpool_t.ap(), wout_t.ap(), wup_t.ap(), wdn_t.ap()
    y, ncs, nps, ncp, npp = y_t.ap(), ncs_t.ap(), nps_t.ap(), ncp_t.ap(), npp_t.ap()

    with contextlib.ExitStack() as st:
        arena = st.enter_context(nc.sbuf_tensor("arena", [128, ARENA_BYTES // 4], F32))
        psum = st.enter_context(nc.psum_tensor("psum", [128, 8, 512], F32))

        PE = Eng(nc, st, nc.tensor, "pe", self_wait=False)
        ACT = Eng(nc, st, nc.scalar, "act")
        DVE = Eng(nc, st, nc.vector, "dve")
        POOL = Eng(nc, st, nc.gpsimd, "pool")
        SP = Eng(nc, st, nc.sync, "sp")

        def view(off, dtype, nelem, parts=128):
            assert off % 4 == 0
            nb = nelem * (4 if dtype == F32 else 2)
            assert nb % 4 == 0 and off + nb <= ARENA_BYTES, (off, nb)
            v = arena[0:parts, off // 4:(off + nb) // 4]
            if dtype != F32:
                v = v.bitcast(dtype)
            return v

        cur = [0]

        def alloc(nbytes):
            o = cur[0]
            cur[0] += (nbytes + 31) // 32 * 32
            return o

        o_identb = alloc(256)
        o_identf = alloc(512)
        o_onesd = alloc(256)
        o_prm = alloc(8 * 35 * 4)
        o_wp = alloc(4 * 2 * 256 * 2)
        o_gb = alloc(D * 4)
        o_stat = alloc(6 * 16 * 4)
        o_epsc = alloc(32)
        o_onec = alloc(32)
        o_negb = alloc(32)
        o_r2 = cur[0]
        o_ht = alloc(16 * NCOL * 2)
        o_win = alloc(2 * 16 * 256 * 2)
        o_xt = alloc(2 * D * 4)
        o_xn = alloc(2 * D * 2)
        o_ct = alloc(8 * NTOK * 2)
        o_dt = alloc(8 * NTOK * 2)
        o_r3a = cur[0]
        o_sig = alloc(2 * NCOL * 4)
        o_xbp = alloc(2 * PC * 4)
        o_sab = alloc(2 * (PC + NSEQ * 19) * 4)
        o_ufp = alloc(8 * 94 * 4)
        o_xbfp = alloc(8 * 94 * 4)
        assert cur[0] - o_r3a >= 2 * 16 * 512 * 2
        o_r3b = cur[0]
        o_up = alloc(8 * PC * 2)
        o_us = alloc(8 * NSEQ * 34 * 2)
        assert cur[0] - o_r3b >= 8 * NTOK * 2
        o_r4 = cur[0]
        o_xbs = alloc(8 * NSEQ * 19 * 4)
        o_ptmp = alloc(2 * NTOK * 2)
        o_outst = o_sig
        assert 2 * CC * 4 <= 2 * NCOL * 4
        o_sld = alloc(2 * CC * 4)
        o_sldb = alloc(CC * 2)
        if cur[0] - o_r4 < 3 * 16 * 256 * 2:
            alloc(3 * 16 * 256 * 2 - (cur[0] - o_r4))
        assert cur[0] - o_r4 >= 3 * 16 * 256 * 2, cur[0] - o_r4
        assert cur[0] <= ARENA_BYTES, cur[0]
        o_diag = o_xt
        o_ln = o_xt + 2 * 31 * 128 * 2
        assert o_ln + 8704 <= o_xn + 2 * D * 2

        identb = view(o_identb, BF16, 128)
        identf = view(o_identf, F32, 128)
        onesd = view(o_onesd, BF16, 128)
        PRM = view(o_prm, F32, 8 * 35).rearrange("p (j k) -> p j k", j=8)
        WP = view(o_wp, BF16, 4 * 2 * 256).rearrange("p (g k d) -> p g k d", g=4, k=2)
        GB = view(o_gb, F32, D)
        STAT = view(o_stat, F32, 6 * 16).rearrange("p (a t) -> p a t", a=6)
        EPSC = view(o_epsc, F32, 1)
        ONEC = view(o_onec, F32, 1)
        NEGB = view(o_negb, F32, 8)
        HT = view(o_ht, BF16, 16 * NCOL).rearrange("p (c t) -> p c t", c=16)
        WIN = [view(o_win + s * 16 * 256 * 2, BF16, 16 * 256).rearrange("p (c e) -> p c e", c=16) for s in range(2)] + \
              [view(o_up + s * 16 * 256 * 2, BF16, 16 * 256).rearrange("p (c e) -> p c e", c=16) for s in range(2)]
        assert 2 * 16 * 256 * 2 <= 8 * PC * 2
        NXT = 6
        XT = [view(o_xt + s * D * 4, F32, D) for s in range(2)] + [view(o_ct + s * D * 4, F32, D) for s in range(4)]
        XN = [view(o_xn + s * D * 2, BF16, D) for s in range(2)]
        x1_offs = [o_win, o_win + 8192] + [o_r3a + k * 8192 for k in range(4)] + [o_xt + k * 8192 for k in range(3)]
        X1 = [view(x1_offs[t], F32, D) for t in range(9)]
        CT = view(o_ct, BF16, 8 * NTOK).rearrange("p (c t) -> p c t", c=8)
        DT = view(o_dt, BF16, 8 * NTOK).rearrange("p (c t) -> p c t", c=8)
        H2T = view(o_ct, BF16, 16 * NTOK).rearrange("p (c t) -> p c t", c=16)
        JUNK = view(o_ct, BF16, D)
        SIG = [view(o_sig + s * NCOL * 4, F32, NCOL) for s in range(2)]
        XBP = [view(o_xbp + s * PC * 4, F32, PC) for s in range(2)]
        SAB = [view(o_sab + s * (PC + NSEQ * 19) * 4, F32, PC + NSEQ * 19) for s in range(2)]
        UFP = view(o_ufp, F32, 8 * 94).rearrange("p (j t) -> p j t", j=8)
        XBFP = view(o_xbfp, F32, 8 * 94).rearrange("p (j t) -> p j t", j=8)
        WOUT = [view(o_ht + s * 16 * 512 * 2, BF16, 16 * 512).rearrange("p (c e) -> p c e", c=16) for s in range(2)]
        WD = view(o_ht, BF16, 8 * D).rearrange("p (c e) -> p c e", c=8)
        UP = view(o_up, BF16, 8 * PC).rearrange("p (j t) -> p j t", j=8)
        US = view(o_us, BF16, 8 * NSEQ * 34).rearrange("p (j s t) -> p j s t", j=8, s=NSEQ)
        AT = view(o_r3b, BF16, 8 * NTOK).rearrange("p (c t) -> p c t", c=8)
        NXN2 = 6
        XN2 = [view(o_r3b + s * D * 2, BF16, D) for s in range(NXN2)]
        assert NXN2 * D * 2 <= 8 * PC * 2 + 8 * NSEQ * 34 * 2
        XBS = view(o_xbs, F32, 8 * NSEQ * 19).rearrange("p (j s t) -> p j s t", j=8, s=NSEQ)
        PTMP = [view(o_ptmp + s * NTOK * 2, BF16, NTOK) for s in range(2)]
        OUTST = [view(o_outst + s * CC * 4, F32, CC) for s in range(2)]
        SLD = [view(o_sld + s * CC * 4, F32, CC) for s in range(2)]
        SLDB = view(o_sldb, BF16, CC)
        PRMRAW = view(o_ufp, F32, CC)
        SLDX = [SLD[0], SLD[1], view(o_sig, F32, CC), view(o_sig + 4096, F32, CC),
                view(o_sab, F32, CC), view(o_sab + 4096, F32, CC)]
        NWU = 3
        WU = [view(o_r4 + s * 16 * 256 * 2, BF16, 16 * 256).rearrange("p (c e) -> p c e", c=16) for s in range(NWU)]
        RELU = [view(o_r3b + 8 * NTOK * 2 + s * 512 * 4, F32, 512) for s in range(3)]
        assert 8 * NTOK * 2 + 3 * 512 * 4 <= 8 * PC * 2 + 8 * NSEQ * 34 * 2
        DIAG = [view(o_diag + s * 31 * 128 * 2, BF16, 31 * 128).rearrange("p (k m) -> p k m", k=31) for s in range(2)]
        C32 = [view(o_ln + s * 2048, F32, 512) for s in range(2)] + [view(o_ptmp + s * 2048, F32, 512) for s in range(2)]
        CB = view(o_ln + 4096, BF16, 512)
        CSQ = view(o_ln + 5120, BF16, 512)
        M2 = view(o_ln + 6144, F32, 512)
        VV = view(o_sld, F32, 512)
        SG = view(o_sld + 2048, F32, 512)
        VVS = [VV, view(o_sld + 4096, F32, 512)]

        R = {}

        def res(name):
            if name not in R:
                R[name] = Res()
            return R[name]

        banks = [Res() for _ in range(8)]
        bank_ctr = [0]

        held = set()

        def next_bank(hold=False):
            while True:
                b = bank_ctr[0] % 8
                bank_ctr[0] += 1
                if b not in held:
                    break
            if hold:
                held.add(b)
            return b

        def pbank(b, parts=128, n=512):
            return psum[0:parts, b, 0:n]

        def pbank_bf(b, parts=128):
            return psum[0:parts, b, :].bitcast(BF16)

        dsems = {}

        def dsem(name):
            if name not in dsems:
                dsems[name] = DmaSem(nc, st, name)
            return dsems[name]

        r_identf, r_identb, r_onesd = res("identf"), res("identb"), res("onesd")
        r_stats = [[res("stat%d_%d" % (ph, t)) for t in range(9)] for ph in range(3)]
        r_stat_all = [r for l in r_stats for r in l]
        op(POOL, lambda: nc.gpsimd.memset(identf, 1.0), writes=[r_identf])
        op(POOL, lambda: nc.gpsimd.affine_select(out=identf, in_=identf, pattern=[[-1, 128]],
                                                 compare_op=ALU.is_ge, fill=0.0, base=0, channel_multiplier=1),
           reads=[r_identf], writes=[r_identf])
        op(POOL, lambda: nc.gpsimd.affine_select(out=identf, in_=identf, pattern=[[1, 128]],
                                                 compare_op=ALU.is_ge, fill=0.0, base=0, channel_multiplier=-1),
           reads=[r_identf], writes=[r_identf])
        op(POOL, lambda: nc.gpsimd.tensor_copy(identb, identf), reads=[r_identf], writes=[r_identb])
        op(POOL, lambda: nc.gpsimd.memset(onesd, 1.0 / 128.0), writes=[r_onesd])
        op(POOL, lambda: nc.gpsimd.memset(STAT, 0.0), writes=r_stat_all)
        op(POOL, lambda: nc.gpsimd.memset(EPSC, EPS), writes=[r_onesd])
        op(POOL, lambda: nc.gpsimd.memset(ONEC, 1.0), writes=[r_onesd])
        ACT.wait(r_onesd.w)

        r_gb = res("gb")

        def load_g(idx):
            dma(SP, dsem("gb"), GB, bass.AP(gs_t, idx * D, [[0, 128], [1, D]]), writes=[r_gb])

        r_prmraw, r_prm, r_wp = res("prmraw"), res("prm"), res("wp")
        r_xt = [res("xt%d" % i) for i in range(NXT)]
        r_xnr = [res("xn0"), res("xn1")]
        r_ht = res("ht")

        A0_ORDER = [0, 1, 2, 3, 4, 5, 6, 7, 8]

        def load_xt(pos):
            t = A0_ORDER[pos]
            nrows = 128 if t < 8 else 94
            nld = 128 if t < 8 else 96
            dma(SP, dsem("xt%d" % (pos % NXT)), XT[pos % NXT][0:nld, :], xin[128 * t:128 * t + nld, :],
                writes=[r_xt[pos % NXT]])
        load_xt(0)
        load_g(0)
        dma(SP, dsem("prmraw"), PRMRAW[0:35, :], prm_d, writes=[r_prmraw])
        for pos in range(1, NXT):
            load_xt(pos)
        r_sldx = [res("sldx%d" % i) for i in range(6)]
        for q in range(4):
            dma(SP, dsem("sldx%d" % q), SLDX[q][0:120, :], sc[4 * q:4 * q + 4].rearrange("s r c -> (s r) c"),
                writes=[r_sldx[q]])
        for q in range(2):
            dma(SP, dsem("sldx%d" % (4 + q)), SLDX[4 + q][0:120, :], sp[8 * q:8 * q + 8].rearrange("s r c -> (s r) c"),
                writes=[r_sldx[4 + q]])
        dma(POOL, dsem("wp"), WP, w_pool.rearrange("g (k p) d -> p g k d", p=128), writes=[r_wp])

        b = next_bank()

        def f_prm_tr():
            ins = None
            for j in range(8):
                ins = nc.tensor.transpose(psum[:, b, j * 35:(j + 1) * 35], PRMRAW[0:35, j * 128:(j + 1) * 128],
                                          identf[0:35, 0:35])
            return ins
        op(PE, f_prm_tr, reads=[r_prmraw, r_identf], writes=[banks[b]])
        op(ACT, lambda: nc.scalar.copy(PRM, psum[:, b, 0:280].rearrange("p (j k) -> p j k", j=8)),
           reads=[banks[b]], writes=[r_prm])
        op(DVE, lambda: nc.vector.tensor_scalar(out=NEGB, in0=PRM[:, :, 33], scalar1=-1.0, scalar2=None, op0=ALU.mult),
           reads=[r_prm], writes=[r_prm])

        r_sldb, r_xbs = res("sldb"), res("xbs")
        r_us = [res("us%d" % j) for j in range(8)]
        r_up = [res("up%d" % j) for j in range(8)]
        def conv_hist(q):
          if True:
            op(ACT, lambda: nc.scalar.copy(SLDB[0:120, :], SLDX[q][0:120, :]), reads=[r_sldx[q]], writes=[r_sldb])
            b = next_bank()
            bb = pbank_bf(b)

            def f_tr():
                ins = None
                for j in range(8):
                    ins = nc.tensor.transpose(bb[:, j * 120:(j + 1) * 120], SLDB[0:120, j * 128:(j + 1) * 128],
                                              identb[0:120, 0:120])
                return ins
            op(PE, f_tr, reads=[r_sldb, r_identb], writes=[banks[b]])
            op(ACT, lambda: nc.scalar.copy(US[:, :, 4 * q:4 * q + 4, 0:30],
                                           bb[:, 0:960].rearrange("p (j s r) -> p j s r", j=8, s=4)),
               reads=[banks[b]], writes=r_us)

        def pool_hist():
          for q in range(2):
            for hh in range(2):
                b = next_bank()

                def f_tr():
                    ins = None
                    for jj in range(4):
                        j = hh * 4 + jj
                        ins = nc.tensor.transpose(psum[:, b, jj * 120:(jj + 1) * 120],
                                                  SLDX[4 + q][0:120, j * 128:(j + 1) * 128], identf[0:120, 0:120])
                    return ins
                op(PE, f_tr, reads=[r_sldx[4 + q], r_identf], writes=[banks[b]])
                op(ACT, lambda: nc.scalar.copy(XBS[:, hh * 4:hh * 4 + 4, 8 * q:8 * q + 8, 0:15],
                                               psum[:, b, 0:480].rearrange("p (j s r) -> p j s r", j=4, s=8)),
                   reads=[banks[b]], writes=[r_xbs])

        def norm_a(t, src, nrows, phase, xn_slot, r_src, r_xn, part="both"):
            if part == "both":
                norm_a(t, src, nrows, phase, xn_slot, r_src, r_xn, "act")
                norm_a(t, src, nrows, phase, xn_slot, r_src, r_xn, "dve")
                return
            ss = STAT[0:nrows, 2 * phase, t:t + 1]
            rs = STAT[0:nrows, 2 * phase + 1, t:t + 1]
            r_stat = r_stats[phase][t]
            if part == "act":
                op(ACT, lambda: nc.scalar.activation(out=xn_slot[0:nrows, :], in_=src[0:nrows, :], func=AF.Square,
                                                     accum_out=ss), reads=[r_src], writes=[r_xn, r_stat])
                op(ACT, lambda: nc.scalar.activation(out=rs, in_=ss, func=AF.Ln, scale=1.0 / D, bias=EPSC[0:nrows, :]),
                   reads=[r_stat], writes=[r_stat])
                op(ACT, lambda: nc.scalar.activation(out=rs, in_=rs, func=AF.Exp, scale=-0.5),
                   reads=[r_stat], writes=[r_stat])
                return
            op(DVE, lambda: nc.vector.scalar_tensor_tensor(out=xn_slot[0:nrows, :], in0=src[0:nrows, :], scalar=rs,
                                                           in1=GB[0:nrows, :], op0=ALU.mult, op1=ALU.mult),
               reads=[r_src, r_stat, r_gb], writes=[r_xn])

        def norm_b(nrows, xn_slot, r_xn, evac):
            for half in range(2):
                b = next_bank()
                bb = pbank_bf(b)

                def f_tr():
                    ins = None
                    for c in range(8):
                        dc = half * 8 + c
                        ins = nc.tensor.transpose(bb[:, c * 128:c * 128 + nrows],
                                                  xn_slot[0:nrows, dc * 128:(dc + 1) * 128],
                                                  identb[0:nrows, 0:nrows])
                    return ins
                op(PE, f_tr, reads=[r_xn, r_identb], writes=[banks[b]])
                evac(b, bb, half, ACT if half == 0 else DVE)

        def cp(eng, out, in_):
            return nc.scalar.copy(out, in_) if eng is ACT else nc.vector.tensor_copy(out, in_)

        def a0_a(pos):
            t = A0_ORDER[pos]
            nrows = 128 if t < 8 else 94
            norm_a(t, XT[pos % NXT], nrows, 0, XN[pos % 2], r_xt[pos % NXT], r_xnr[pos % 2])

        def a0_b(pos):
            t = A0_ORDER[pos]
            nrows = 128 if t < 8 else 94

            def evac(b, bb, half, eng):
                src3 = bb.rearrange("p (c t) -> p c t", c=8)
                if t < 8:
                    op(eng, lambda: cp(eng, HT[:, half * 8:half * 8 + 8, NH + 128 * t:NH + 128 * t + 128],
                                       src3[:, :, 0:128]), reads=[banks[b]], writes=[r_ht])
                else:
                    op(eng, lambda: cp(eng, HT[:, half * 8:half * 8 + 8, PC:PC + NS], src3[:, :, 0:NS]),
                       reads=[banks[b]], writes=[r_ht])
                    op(eng, lambda: cp(eng, HT[:, half * 8:half * 8 + 8, 0:NH], src3[:, :, NS:NS + NH]),
                       reads=[banks[b]], writes=[r_ht])
            norm_b(nrows, XN[pos % 2], r_xnr[pos % 2], evac)

        a0_a(0)
        for t in range(9):
            if t + 1 < 9:
                a0_a(t + 1)
            if t + NXT < 9:
                load_xt(t + NXT)
            a0_b(t)
        pool_hist()

        TB = [(0, 512), (512, 512), (1024, 94)]
        r_win = [res("win%d" % i) for i in range(4)]
        r_sig = [res("sig0"), res("sig1")]
        r_xbp = [res("xbp0"), res("xbp1")]
        r_sab = [res("sab0"), res("sab1")]
        r_ufp, r_xbfp = res("ufp"), res("xbfp")
        r_dt = [res("dt%d" % j) for j in range(8)]
        r_ct = [res("ct%d" % j) for j in range(8)]
        r_ptmp = [res("ptmp0"), res("ptmp1")]
        r_diag = [res("diag0"), res("diag1")]
        r_ln = {k: res("ln_" + k) for k in ("c32a", "c32b", "cb", "csq", "m2", "vv", "sg", "zz")}
        win_ctr = [0]

        def load_win(blk, slot=None):
            if slot is None:
                s_ = win_ctr[0] % 2
                win_ctr[0] += 1
            else:
                s_ = slot
            dma(POOL, dsem("win%d" % s_), WIN[s_],
                w_in[:, 256 * blk:256 * blk + 256].rearrange("(c p) e -> p c e", p=128), writes=[r_win[s_]])
            return s_

        def win_chunk(s_, hh, consumer):
            for tb, (c0, n) in enumerate(TB):
                b = next_bank()

                def f_mm():
                    ins = None
                    for dc in range(16):
                        ins = nc.tensor.matmul(pbank(b, 128, n), WIN[s_][:, dc, hh * 128:(hh + 1) * 128],
                                               HT[:, dc, c0:c0 + n], start=(dc == 0), stop=(dc == 15))
                    return ins
                op(PE, f_mm, reads=[r_win[s_], r_ht], writes=[banks[b]])
                consumer(tb, c0, n, b)

        def pooling(j, xs):
            w = WINDOWS[j // 2]
            steps = {2: 1, 4: 2, 8: 3, 16: 4}[w]
            X = XBP[xs]
            XS = XBS[:, j]
            srcP, srcS = X, XS
            r_src = r_xbp[xs]
            for i in range(1, steps + 1):
                sh = 2 ** (i - 1)
                lo = 2 ** i - 1
                d_ = (i - 1) % 2
                dstP = SAB[d_][:, 0:PC]
                dstS = SAB[d_][:, PC:PC + NSEQ * 19].rearrange("p (s t) -> p s t", s=NSEQ)
                rd = [r_src, r_xbs] if i == 1 else [r_src]
                op(DVE, lambda: nc.vector.tensor_tensor(out=dstP[:, lo:PC], in0=srcP[:, lo:PC], in1=srcP[:, lo - sh:PC - sh],
                                                        op=ALU.add), reads=rd, writes=[r_sab[d_]])
                op(DVE, lambda: nc.vector.tensor_tensor(out=dstS[:, :, lo:19], in0=srcS[:, :, lo:19],
                                                        in1=srcS[:, :, lo - sh:19 - sh], op=ALU.add),
                   reads=rd, writes=[r_sab[d_]])
                srcP, srcS, r_src = dstP, dstS, r_sab[d_]
            op(DVE, lambda: nc.vector.scalar_tensor_tensor(out=DT[:, j, 0:NPR], in0=srcP[:, NH:PC], scalar=1.0 / w,
                                                           in1=X[:, NH:PC], op0=ALU.mult, op1=ALU.subtract),
               reads=[r_src, r_xbp[xs]], writes=[r_dt[j]])
            op(DVE, lambda: nc.vector.scalar_tensor_tensor(
                out=DT[:, j, NPR:NTOK].rearrange("p (s t) -> p s t", s=NSEQ), in0=srcS[:, :, 15:19], scalar=1.0 / w,
                in1=XS[:, :, 15:19], op0=ALU.mult, op1=ALU.subtract),
               reads=[r_src, r_xbs, r_xbp[xs]], writes=[r_dt[j]])

        def pool_mm(gi):
            TBK = [(0, 512), (512, 512), (1024, 64)]
            for oc in range(2):
                jo = 2 * gi + oc
                for (c0, n) in TBK:
                    b = next_bank()

                    def f_mm():
                        ins = None
                        for kc in range(2):
                            ins = nc.tensor.matmul(pbank(b, 128, n), WP[:, gi, kc, oc * 128:(oc + 1) * 128],
                                                   DT[:, 2 * gi + kc, c0:c0 + n], start=(kc == 0), stop=(kc == 1))
                        return ins
                    op(PE, f_mm, reads=[r_wp, r_dt[2 * gi], r_dt[2 * gi + 1]], writes=[banks[b]])
                    if oc == 0:
                        op(ACT, lambda: nc.scalar.activation(out=PTMP[gi % 2][:, c0:c0 + n], in_=pbank(b, 128, n),
                                                             func=AF.Identity, scale=PRM[:, jo, 34:35]),
                           reads=[banks[b], r_prm], writes=[r_ptmp[gi % 2]])
                    else:
                        op(ACT, lambda: nc.scalar.activation(out=DT[:, jo, c0:c0 + n], in_=pbank(b, 128, n),
                                                             func=AF.Identity, scale=PRM[:, jo, 34:35]),
                           reads=[banks[b], r_prm], writes=[r_dt[jo]])
            op(DVE, lambda: nc.vector.tensor_copy(DT[:, 2 * gi, :], PTMP[gi % 2]), reads=[r_ptmp[gi % 2]],
               writes=[r_dt[2 * gi]])

        def build_diag(j):
            op(DVE, lambda: nc.vector.tensor_tensor(out=DIAG[j % 2], in0=bc(identb, [[0, 31], [1, 128]]),
                                                    in1=bc(PRM[:, j, 0:31], [[1, 31], [0, 128]]), op=ALU.mult),
               reads=[r_identb, r_prm], writes=[r_diag[j % 2]], free=[r_xt[0], r_xt[1], r_xnr[0], r_xnr[1]])

        def conv_chunk(j):
            ds_ = j % 2
            if j + 1 < 8:
                build_diag(j + 1)
            TBK = [(0, 512), (512, 512), (1024, 64)]
            units = []
            for tb, (c0, n) in enumerate(TBK):
                b = next_bank(hold=True)

                def f_mm():
                    ins = None
                    for k in range(31):
                        if tb < 2:
                            rhs = UP[:, j, c0 + k:c0 + k + n]
                            out = pbank(b, 128, n)
                        else:
                            rhs = US[:, j, :, k:k + 4]
                            out = pbank(b, 128, n).rearrange("p (s t) -> p s t", s=NSEQ)
                        ins = nc.tensor.matmul(out, DIAG[ds_][:, k, :], rhs, start=(k == 0), stop=(k == 30))
                    return ins
                op(PE, f_mm, reads=[r_diag[ds_], r_up[j], r_us[j]], writes=[banks[b]])
                ln_push((j, (tb, c0, n, b)))

        ln_ctr = [0]
        r_c32 = [res("c32_%d" % i) for i in range(4)]
        r_vv = [res("vv0"), res("vv1")]

        def ln_s1(item):
            j, tb, c0, n, b = item
            k = ln_ctr[0]
            ln_ctr[0] += 1
            c32 = C32[k % 4][:, 0:n]
            rc = r_c32[k % 4]
            bias = PRM[:, j, 31:32]
            fr = [r_ptmp[0], r_ptmp[1]] if k % 4 >= 2 else []
            op(ACT, lambda: nc.scalar.activation(out=c32, in_=pbank(b, 128, n), func=AF.Identity, bias=bias),
               reads=[banks[b], r_prm], writes=[rc], free=fr)
            op(DVE, lambda: nc.vector.tensor_copy(CB[:, 0:n], c32), reads=[rc], writes=[r_ln["cb"]])
            op(ACT, lambda: nc.scalar.activation(out=CSQ[:, 0:n], in_=pbank(b, 128, n), func=AF.Square, bias=bias),
               reads=[banks[b], r_prm], writes=[r_ln["csq"]])
            held.discard(b)
            bm, be = next_bank(hold=True), next_bank(hold=True)
            op(PE, lambda: nc.tensor.matmul(pbank(bm, 128, n), onesd, CB[:, 0:n], start=True, stop=True),
               reads=[r_onesd, r_ln["cb"]], writes=[banks[bm]])
            op(PE, lambda: nc.tensor.matmul(pbank(be, 128, n), onesd, CSQ[:, 0:n], start=True, stop=True),
               reads=[r_onesd, r_ln["csq"]], writes=[banks[be]])
            return (j, c0, n, k, bm, be)

        def ln_s2(item):
            j, c0, n, k, bm, be = item
            c32, rc = C32[k % 4][:, 0:n], r_c32[k % 4]
            vv, rv = VVS[k % 2][:, 0:n], r_vv[k % 2]
            op(ACT, lambda: nc.scalar.activation(out=M2[:, 0:n], in_=pbank(bm, 128, n), func=AF.Square),
               reads=[banks[bm]], writes=[r_ln["m2"]])
            op(DVE, lambda: nc.vector.tensor_tensor(out=vv, in0=pbank(be, 128, n), in1=M2[:, 0:n], op=ALU.subtract),
               reads=[banks[be], r_ln["m2"]], writes=[rv])
            op(DVE, lambda: nc.vector.tensor_tensor(out=c32, in0=c32, in1=pbank(bm, 128, n), op=ALU.subtract),
               reads=[rc, banks[bm]], writes=[rc])
            held.discard(bm)
            held.discard(be)
            return (j, c0, n, k)

        def ln_s3(item):
            j, c0, n, k = item
            c32, rc = C32[k % 4][:, 0:n], r_c32[k % 4]
            vv, rv = VVS[k % 2][:, 0:n], r_vv[k % 2]
            op(ACT, lambda: nc.scalar.activation(out=vv, in_=vv, func=AF.Ln, bias=EPSC), reads=[rv], writes=[rv])
            op(ACT, lambda: nc.scalar.activation(out=vv, in_=vv, func=AF.Exp, scale=-0.5), reads=[rv], writes=[rv])
            op(DVE, lambda: nc.vector.scalar_tensor_tensor(out=c32, in0=c32, scalar=PRM[:, j, 32:33], in1=vv,
                                                           op0=ALU.mult, op1=ALU.mult),
               reads=[rc, rv, r_prm], writes=[rc])
            return (j, c0, n, k)

        def ln_s4(item):
            j, c0, n, k = item
            c32, rc = C32[k % 4][:, 0:n], r_c32[k % 4]
            lb = PRM[:, j, 33:34]
            sg = SG[:, 0:n]
            op(ACT, lambda: nc.scalar.activation(out=sg, in_=c32, func=AF.Exp, scale=-1.0, bias=NEGB[:, j:j + 1]),
               reads=[rc, r_prm], writes=[r_ln["sg"]])
            op(ACT, lambda: nc.scalar.activation(out=sg, in_=sg, func=AF.Ln, bias=ONEC), reads=[r_ln["sg"]],
               writes=[r_ln["sg"]])
            op(ACT, lambda: nc.scalar.activation(out=sg, in_=sg, func=AF.Exp, scale=-1.0), reads=[r_ln["sg"]],
               writes=[r_ln["sg"]])
            op(DVE, lambda: nc.vector.scalar_tensor_tensor(out=CT[:, j, c0:c0 + n], in0=c32, scalar=lb, in1=SG[:, 0:n],
                                                           op0=ALU.add, op1=ALU.mult),
               reads=[rc, r_ln["sg"], r_prm], writes=[r_ct[j]])

        lq = [[], [], [], []]

        def ln_step(lag):
            if lq[3]:
                ln_s4(lq[3].pop(0))
            if lq[2]:
                lq[3].append(ln_s3(lq[2].pop(0)))
            if lq[1]:
                lq[2].append(ln_s2(lq[1].pop(0)))
            if len(lq[0]) > lag:
                lq[1].append(ln_s1(lq[0].pop(0)))

        def ln_push(item):
            j, (tb, c0, n, b) = item
            lq[0].append((j, tb, c0, n, b))
            ln_step(1)

        def ln_drain():
            while any(lq):
                ln_step(0)

        build_diag(0)
        xb_slots = [load_win(8), load_win(9), load_win(10, slot=2), load_win(11, slot=3)]
        for blk in range(8, 12):
            s_ = xb_slots[blk - 8]
            for hh in range(2):
                j = (blk - 8) * 2 + hh
                xs = j % 2

                def cons_xb(tb, c0, n, b, j=j, xs=xs):
                    if tb < 2:
                        op(ACT, lambda: nc.scalar.copy(XBP[xs][:, c0:c0 + n], pbank(b, 128, n)),
                           reads=[banks[b]], writes=[r_xbp[xs]])
                    else:
                        op(ACT, lambda: nc.scalar.copy(XBFP[:, j, :], pbank(b, 128, 94)), reads=[banks[b]],
                           writes=[r_xbfp])
                        op(ACT, lambda: nc.scalar.copy(XBP[xs][:, 1024:PC], pbank(b, 128, 30)), reads=[banks[b]],
                           writes=[r_xbp[xs]])
                        op(ACT, lambda: nc.scalar.copy(XBS[:, j, :, 15:19],
                                                       psum[:, b, 30:94].rearrange("p (s t) -> p s t", s=NSEQ)),
                           reads=[banks[b]], writes=[r_xbs])
                win_chunk(s_, hh, cons_xb)
                if j < 4:
                    conv_hist(j)
                pooling(j, xs)
        r_x1 = [res("x1_%d" % t) for t in range(9)]
        r_wout = [res("wout0"), res("wout1")]
        TT = [(128 * t, 128) for t in range(8)] + [(1024, 64)]

        def load_wout(nb):
            s_ = nb % 2
            dma(POOL, dsem("wout%d" % s_), WOUT[s_],
                w_out[:, 512 * nb:512 * nb + 512].rearrange("(c p) e -> p c e", p=128),
                writes=[r_wout[s_]], free=[r_ht])

        def load_x1(t, free):
            r0, m_ = TT[t]
            dma(SP, dsem("x1_%d" % t), X1[t][0:m_, :], xin[r0:r0 + m_, :], writes=[r_x1[t]], free=free)

        conv_units = {}
        ln1 = {}

        def do_conv(jlist):
            for j in jlist:
                conv_chunk(j)

        for m in range(4):
            s_g = load_win(4 + m)
            for hh in range(2):
                j = 2 * m + hh

                def cons_gate(tb, c0, n, b, j=j):
                    sg = SIG[j % 2][:, c0:c0 + n]
                    op(ACT, lambda: nc.scalar.activation(out=sg, in_=pbank(b, 128, n), func=AF.Exp, scale=-1.0),
                       reads=[banks[b]], writes=[r_sig[j % 2]])
                    op(ACT, lambda: nc.scalar.activation(out=sg, in_=sg, func=AF.Ln, bias=ONEC), reads=[r_sig[j % 2]],
                       writes=[r_sig[j % 2]])
                    op(ACT, lambda: nc.scalar.activation(out=sg, in_=sg, func=AF.Exp, scale=-1.0), reads=[r_sig[j % 2]],
                       writes=[r_sig[j % 2]])
                win_chunk(s_g, hh, cons_gate)
            s_a = load_win(m)
            for hh in range(2):
                j = 2 * m + hh

                def cons_a(tb, c0, n, b, j=j):
                    if tb < 2:
                        op(DVE, lambda: nc.vector.tensor_tensor(out=UP[:, j, c0:c0 + n], in0=pbank(b, 128, n),
                                                                in1=SIG[j % 2][:, c0:c0 + n], op=ALU.mult),
                           reads=[banks[b], r_sig[j % 2]], writes=[r_up[j]])
                    else:
                        op(DVE, lambda: nc.vector.tensor_tensor(out=UFP[:, j, :], in0=pbank(b, 128, 94),
                                                                in1=SIG[j % 2][:, 1024:NCOL], op=ALU.mult),
                           reads=[banks[b], r_sig[j % 2]], writes=[r_ufp])
                        op(DVE, lambda: nc.vector.tensor_copy(UP[:, j, 1024:PC], UFP[:, j, 0:30]), reads=[r_ufp],
                           writes=[r_up[j]])
                        op(DVE, lambda: nc.vector.tensor_copy(US[:, j, :, 30:34],
                                                              UFP[:, j, 30:94].rearrange("p (s t) -> p s t", s=NSEQ)),
                           reads=[r_ufp], writes=[r_us[j]])
                win_chunk(s_a, hh, cons_a)
            if m == 0:
                pool_mm(0)
                pool_mm(1)
            if m == 1:
                pool_mm(2)
                pool_mm(3)
            if m == 3:
                load_wout(0)
                load_wout(1)
                load_x1(0, [r_win[0], r_win[1]])
                load_x1(1, [r_win[0], r_win[1]])
            if m >= 1:
                do_conv([2 * m - 1])
            do_conv([2 * m])

        def state_out(SRC, r_src, slot, prompt_rows, dst_p, dst_s, s_off, name):
            ob = [next_bank(), next_bank()]
            for hh in range(2):
                b = ob[hh]

                def f_tr():
                    ins = None
                    for jj in range(4):
                        ins = nc.tensor.transpose(psum[0:94, b, jj * 128:(jj + 1) * 128], SRC[:, hh * 4 + jj, :], identf)
                    return ins
                op(PE, f_tr, reads=[r_src, r_identf], writes=[banks[b]])
                op(ACT, lambda: nc.scalar.copy(OUTST[slot][0:94, hh * 512:(hh + 1) * 512], psum[0:94, b, :]),
                   reads=[banks[b]], writes=[res("outst%d" % slot)], free=[r_sig[0], r_sig[1]])
            r_o = res("outst%d" % slot)
            lo = 30 - prompt_rows
            dma(SP, dsem(name), dst_p, OUTST[slot][lo:30, :], reads=[r_o])
            for s in range(NSEQ):
                dma(SP, dsem(name), dst_s[s, s_off:s_off + 4, :], OUTST[slot][30 + 4 * s:34 + 4 * s, :], reads=[r_o])

        state_out(UFP, r_ufp, 0, 30, ncp, ncs, 26, "st_u")
        state_out(XBFP, r_xbfp, 1, 15, npp, nps, 11, "st_x")
        r3a_users = [r_sig[0], r_sig[1], r_xbp[0], r_xbp[1], r_sab[0], r_sab[1], r_ufp, r_xbfp,
                     res("outst0"), res("outst1"), r_prmraw] + r_sldx
        for t in range(2, 6):
            load_x1(t, r3a_users)
        do_conv([7])
        ln_step(0)
        NPRE = 5
        PRE_EC = [8 + i for i in range(8)] + [0, 1, 2, 3, 4, 5]
        pre_banks = {}

        def cp_chunk(ec):
            return (CT[:, ec], r_ct[ec]) if ec < 8 else (DT[:, ec - 8], r_dt[ec - 8])

        for t in range(NPRE):
            r0, m_ = TT[t]
            b = next_bank(hold=True)
            pre_banks[t] = b

            def f_mm():
                ins = None
                for i, ec in enumerate(PRE_EC):
                    a_, _ = cp_chunk(ec)
                    ins = nc.tensor.matmul(pbank(b, m_, 512), a_[:, r0:r0 + m_], WOUT[0][:, ec, :],
                                           start=(i == 0), stop=False)
                return ins
            op(PE, f_mm, reads=[r_wout[0]] + [r_ct[e] for e in range(6)] + r_dt, writes=[banks[b]])
        ln_drain()
        xtxn_users = [r_xt[0], r_xt[1], r_xnr[0], r_xnr[1], r_diag[0], r_diag[1]] + \
            [r_ln[k] for k in ("cb", "csq", "m2")] + [r_c32[0], r_c32[1]]
        for t in range(6, 9):
            load_x1(t, xtxn_users)
        dma(SP, dsem("hist"), ncs[:, 0:26, :], sc[:, 4:30, :])
        dma(SP, dsem("hist"), nps[:, 0:11, :], sp[:, 4:15, :])

        if debug:
            dma(SP, dsem("dbg"), dbg_ct, view(o_ct, BF16, 8 * NTOK), reads=r_ct)
            dma(SP, dsem("dbg"), dbg_dt, view(o_dt, BF16, 8 * NTOK), reads=r_dt)
        r_h2t = [[res("h2t%d_%d" % (t, h)) for h in range(2)] for t in range(9)]
        r_h2t_all = [r for l in r_h2t for r in l]
        r_xn2 = [res("xn2_%d" % i) for i in range(NXN2)]
        r1_users = r_ct + r_dt
        r3b_users = r_up + r_us
        load_g(1)

        def d0_a(t, part="both"):
            r0, m_ = TT[t]
            norm_a(t, X1[t], m_, 1, XN2[t % NXN2], r_x1[t], r_xn2[t % NXN2], part)

        def d0_b(t):
            r0, m_ = TT[t]

            def evac(b, bb, half, eng):
                src3 = bb.rearrange("p (c t) -> p c t", c=8)
                op(eng, lambda: cp(eng, H2T[:, half * 8:half * 8 + 8, r0:r0 + m_], src3[:, :, 0:m_]),
                   reads=[banks[b]], writes=[r_h2t[t][half]], free=r1_users)
            norm_b(m_, XN2[t % NXN2], r_xn2[t % NXN2], evac)

        for nb in range(4):
            s_ = nb % 2
            for t in range(9):
                r0, m_ = TT[t]
                pre = (nb == 0 and t in pre_banks)
                b = pre_banks[t] if pre else next_bank()
                ecs = [6, 7] if pre else list(range(16))

                def f_mm():
                    ins = None
                    for i, ec in enumerate(ecs):
                        a_, _ = cp_chunk(ec)
                        ins = nc.tensor.matmul(pbank(b, m_, 512), a_[:, r0:r0 + m_], WOUT[s_][:, ec, :],
                                               start=(i == 0 and not pre), stop=(i == len(ecs) - 1))
                    return ins
                op(PE, f_mm, reads=[r_wout[s_]] + r_ct + r_dt, writes=[banks[b]])
                xs_ = X1[t][0:m_, nb * 512:(nb + 1) * 512]
                op(DVE, lambda: nc.vector.tensor_tensor(out=xs_, in0=pbank(b, m_, 512), in1=xs_, op=ALU.add),
                   reads=[banks[b]], writes=[r_x1[t]])
                held.discard(b)
                if nb == 3 and t < NXN2:
                    if t == 0:
                        for rr in r3b_users:
                            DVE.wait(rr.w, *rr.r.values())
                            ACT.wait(rr.w, *rr.r.values())
                    d0_a(t, "act")
                if nb == 3 and 1 <= t <= NXN2:
                    d0_a(t - 1, "dve")
            if nb + 2 < 4:
                load_wout(nb + 2)

        if debug:
            for t in range(9):
                dma(SP, dsem("dbg"), dbg_x1[t], X1[t], reads=[r_x1[t]])
        for t in range(9):
            d0_b(t)
            if t + NXN2 < 9:
                d0_a(t + NXN2)

        r_wu = [res("wu%d" % i) for i in range(NWU)]
        r_wd = res("wd")
        r_at = [res("at%d" % i) for i in range(8)]
        r_relu = [res("relu%d" % i) for i in range(3)]
        r4_users = [r_xbs, r_ptmp[0], r_ptmp[1], r_sldx[0], r_sldx[1], r_sldb,
                    r_vv[0], r_vv[1], r_ln["sg"], r_c32[2], r_c32[3]]
        wu_ctr = [0]
        relu_ctr = [0]
        TBM = [(0, 363), (363, 363), (726, 362)]

        def load_wu(blk):
            s_ = blk % NWU
            dma(POOL, dsem("wu%d" % s_), WU[s_],
                w_up[:, 256 * blk:256 * blk + 256].rearrange("(c p) e -> p c e", p=128),
                writes=[r_wu[s_]], free=r4_users)

        def load_wd(g):
            dma(POOL, dsem("wd"), WD, w_dn[1024 * g:1024 * g + 1024, :].rearrange("(c p) e -> p c e", p=128),
                writes=[r_wd], free=[r_wout[0], r_wout[1], r_ht])

        for blk in range(NWU):
            load_wu(blk)
        load_g(2)
        def final_scale_store(t):
            r0, m_ = TT[t]
            rs = STAT[0:m_, 5, t:t + 1]
            op(DVE, lambda: nc.vector.scalar_tensor_tensor(out=X1[t][0:m_, :], in0=X1[t][0:m_, :], scalar=rs,
                                                           in1=GB[0:m_, :], op0=ALU.mult, op1=ALU.mult),
               reads=[r_stats[2][t], r_gb], writes=[r_x1[t]])
            dma(SP, dsem("yout"), y[r0:r0 + m_, :], X1[t][0:m_, :], reads=[r_x1[t]])

        for g in range(8):
            for bi in range(4):
                blk = 4 * g + bi
                s_ = blk % NWU
                for hh in range(2):
                    fcl = bi * 2 + hh
                    for (c0, n) in TBM:
                        b = next_bank()

                        def f_mm():
                            ins = None
                            for dc in range(16):
                                ins = nc.tensor.matmul(pbank(b, 128, n), WU[s_][:, dc, hh * 128:(hh + 1) * 128],
                                                       H2T[:, dc, c0:c0 + n], start=(dc == 0), stop=(dc == 15))
                            return ins
                        tl = [r for t_ in range(9) if TT[t_][0] < c0 + n and TT[t_][0] + TT[t_][1] > c0
                              for r in r_h2t[t_]]
                        op(PE, f_mm, reads=[r_wu[s_]] + tl, writes=[banks[b]])
                        rs_ = relu_ctr[0] % 3
                        relu_ctr[0] += 1
                        op(ACT, lambda: nc.scalar.activation(out=RELU[rs_][:, 0:n], in_=pbank(b, 128, n), func=AF.Relu),
                           reads=[banks[b]], writes=[r_relu[rs_]], free=r3b_users + r_xn2)
                        op(DVE, lambda: nc.vector.tensor_tensor(out=AT[:, fcl, c0:c0 + n], in0=RELU[rs_][:, 0:n],
                                                                in1=RELU[rs_][:, 0:n], op=ALU.mult),
                           reads=[r_relu[rs_]], writes=[r_at[fcl]], free=r3b_users + r_xn2)
                if bi == 0:
                    load_wd(g)
                if blk + NWU < 32:
                    load_wu(blk + NWU)
            for t in range(9):
                r0, m_ = TT[t]
                for nb in range(4):
                    b = next_bank()

                    def f_mm():
                        ins = None
                        for fc in range(8):
                            PE.wait(r_at[fc].w)
                            ins = nc.tensor.matmul(pbank(b, m_, 512), AT[:, fc, r0:r0 + m_],
                                                   WD[:, fc, nb * 512:(nb + 1) * 512], start=(fc == 0), stop=(fc == 7))
                        return ins
                    tk = op(PE, f_mm, reads=[r_wd], writes=[banks[b]])
                    for fc in range(8):
                        r_at[fc].r[tk[0]] = tk
                    xs_ = X1[t][0:m_, nb * 512:(nb + 1) * 512]
                    op(DVE, lambda: nc.vector.tensor_tensor(out=xs_, in0=pbank(b, m_, 512), in1=xs_, op=ALU.add),
                       reads=[banks[b]], writes=[r_x1[t]])
                if g == 7:
                    ss = STAT[0:m_, 4, t:t + 1]
                    rs = STAT[0:m_, 5, t:t + 1]
                    r_stat = r_stats[2][t]
                    op(ACT, lambda: nc.scalar.activation(out=JUNK[0:m_, :], in_=X1[t][0:m_, :], func=AF.Square,
                                                         accum_out=ss),
                       reads=[r_x1[t]], writes=r_h2t_all + [r_stat])
                    op(ACT, lambda: nc.scalar.activation(out=rs, in_=ss, func=AF.Ln, scale=1.0 / D,
                                                         bias=EPSC[0:m_, :]), reads=[r_stat], writes=[r_stat])
                    op(ACT, lambda: nc.scalar.activation(out=rs, in_=rs, func=AF.Exp, scale=-0.5),
                       reads=[r_stat], writes=[r_stat])
                    if t >= 1:
                        final_scale_store(t - 1)
        final_scale_store(8)

        for name in ("yout", "st_u", "st_x", "hist") + (("dbg",) if debug else ()):
            d_ = dsems[name]
            SP.h.wait_ge(d_.sem, d_.n)
    return nc


_NC_CACHE = {}


def kernel(x_prompt, x_sample, state_conv, state_pool, meta_tokens, norm_mix_g, w_in, w_dw, b_dw,
           conv_ln_g, conv_ln_b, w_pool, pool_scale, w_out, norm_ffn_g, w_up, w_down, final_norm_g):
    f = lambda a: np.ascontiguousarray(np.asarray(a, dtype=np.float32))
    x_prompt, x_sample, state_conv, state_pool, meta_tokens = map(f, (x_prompt, x_sample, state_conv, state_pool, meta_tokens))
    gs = f(np.stack([np.asarray(norm_mix_g)[0], np.asarray(norm_ffn_g)[0], np.asarray(final_norm_g)], axis=0))
    prm = f(np.concatenate([np.asarray(w_dw)[0], np.asarray(b_dw), np.asarray(conv_ln_g), np.asarray(conv_ln_b),
                            np.asarray(pool_scale)], axis=0))
    w_in_, w_pool_, w_out_, w_up_, w_down_ = f(w_in)[0], f(w_pool)[0], f(w_out)[0], f(w_up)[0], f(w_down)[0]
    in_maps = []
    for i in range(8):
        b, h = i // 2, i % 2
        if h == 0:
            halo = np.concatenate([np.zeros((NH - 16, D), np.float32), meta_tokens], axis=0)
        else:
            halo = x_prompt[b, NPR - NH:NPR]
        xin = np.concatenate([x_prompt[b, h * NPR:(h + 1) * NPR], x_sample[16 * i:16 * i + 16].reshape(NS, D), halo,
                              np.zeros((2, D), np.float32)], axis=0)
        in_maps.append({
            "xin": np.ascontiguousarray(xin), "sc": np.ascontiguousarray(state_conv[0, 16 * i:16 * i + 16]),
            "sp": np.ascontiguousarray(state_pool[0, 16 * i:16 * i + 16]), "gs": gs, "prm": prm,
            "w_in": w_in_, "w_pool": w_pool_, "w_out": w_out_, "w_up": w_up_, "w_down": w_down_,
        })
    if "nc" not in _NC_CACHE:
        _NC_CACHE["nc"] = build()
    res = run_bass_kernel_spmd(_NC_CACHE["nc"], in_maps, core_ids=list(range(8)))
    outs = res.results
    y_prompt = np.zeros((4, 2048, D), np.float32)
    y_sample = np.zeros((128, 4, D), np.float32)
    ncp = np.zeros((1, 4, 30, CC), np.float32)
    npp = np.zeros((1, 4, 15, CC), np.float32)
    ncs = np.zeros((1, 128, 30, CC), np.float32)
    nps = np.zeros((1, 128, 15, CC), np.float32)
    for i in range(8):
        b, h = i // 2, i % 2
        o = outs[i]
        y_prompt[b, h * NPR:(h + 1) * NPR] = o["y"][0:NPR]
        y_sample[16 * i:16 * i + 16] = o["y"][NPR:NTOK].reshape(16, 4, D)
        ncs[0, 16 * i:16 * i + 16] = o["ncs"]
        nps[0, 16 * i:16 * i + 16] = o["nps"]
        if h == 1:
            ncp[0, b] = o["ncp"]
            npp[0, b] = o["npp"]
    return (y_prompt, y_sample, ncp, npp, ncs, nps)
```

```python
import contextlib
import numpy as np
import concourse.bass as bass
import concourse.mybir as mybir
from concourse.bass_utils import run_bass_kernel_spmd

F32 = mybir.dt.float32
BF16 = mybir.dt.bfloat16
AF = mybir.ActivationFunctionType
ALU = mybir.AluOpType

D = 2048
NPR = 1024
NH = 30
NS = 64
NSEQ = 16
PC = NH + NPR
NCOL = PC + NS
NTOK = NPR + NS
CC = 1024
DFF = 8192
EPS = 1e-6
WINDOWS = (2, 4, 8, 16)
ARENA_BYTES = 212000


class Eng:
    def __init__(self, nc, stack, handle, name, self_wait=True):
        self.h = handle
        self.key = name
        self.sem = stack.enter_context(nc.semaphore("sem_" + name))
        self.n = 0
        self.seen = {}
        self.self_wait = self_wait

    def wait(self, *toks):
        for t in toks:
            if t is None:
                continue
            key, sem, val = t
            if key == self.key and not self.self_wait:
                continue
            if self.seen.get(key, 0) >= val:
                continue
            self.h.wait_ge(sem, val)
            self.seen[key] = val

    def done(self, ins):
        ins.then_inc(self.sem, 1)
        self.n += 1
        return (self.key, self.sem, self.n)


class DmaSem:
    def __init__(self, nc, stack, name):
        self.key = "dma_" + name
        self.sem = stack.enter_context(nc.semaphore("dsem_" + name))
        self.n = 0

    def done(self, ins):
        ins.then_inc(self.sem, 16)
        self.n += 16
        return (self.key, self.sem, self.n)


class Res:
    def __init__(self):
        self.w = None
        self.r = {}


def _pre(eng, reads, writes, free=()):
    for r in reads:
        eng.wait(r.w)
    for r in list(writes) + list(free):
        eng.wait(r.w)
        eng.wait(*r.r.values())


def _post(tok, reads, writes):
    for r in reads:
        r.r[tok[0]] = tok
    for r in writes:
        r.w = tok
        r.r = {}


def op(eng, fn, reads=(), writes=(), free=()):
    _pre(eng, reads, writes, free)
    ins = fn()
    tok = eng.done(ins)
    _post(tok, reads, writes)
    return tok


def dma(queue, dsem, out, in_, reads=(), writes=(), free=()):
    _pre(queue, reads, writes, free)
    ins = queue.h.dma_start(out=out, in_=in_)
    tok = dsem.done(ins)
    _post(tok, reads, writes)
    return tok


def bc(ap, dims):
    return bass.AP(ap.tensor, ap.offset, [list(ap.ap[0])] + [list(d) for d in dims])


def build(debug=False):
    nc = bass.Bass("TRN2", target_bir_lowering=False)
    dr = lambda n, s, k="ExternalInput": nc.dram_tensor(n, s, F32, kind=k)
    xin_t = dr("xin", [NCOL + 2, D])
    sc_t = dr("sc", [NSEQ, 30, CC])
    sp_t = dr("sp", [NSEQ, 15, CC])
    gs_t = dr("gs", [3, D])
    prm_t = dr("prm", [35, CC])
    win_t = dr("w_in", [D, 3 * CC])
    wpool_t = dr("w_pool", [4, 256, 256])
    wout_t = dr("w_out", [D, D])
    wup_t = dr("w_up", [D, DFF])
    wdn_t = dr("w_down", [DFF, D])
    y_t = dr("y", [NTOK, D], "ExternalOutput")
    ncs_t = dr("ncs", [NSEQ, 30, CC], "ExternalOutput")
    nps_t = dr("nps", [NSEQ, 15, CC], "ExternalOutput")
    ncp_t = dr("ncp", [30, CC], "ExternalOutput")
    npp_t = dr("npp", [15, CC], "ExternalOutput")
    if debug:
        dbg_ct = nc.dram_tensor("dbg_ct", [128, 8 * NTOK], BF16, kind="ExternalOutput").ap()
        dbg_dt = nc.dram_tensor("dbg_dt", [128, 8 * NTOK], BF16, kind="ExternalOutput").ap()
        dbg_x1 = nc.dram_tensor("dbg_x1", [9, 128, D], F32, kind="ExternalOutput").ap()
    xin, sc, sp, prm_d = xin_t.ap(), sc_t.ap(), sp_t.ap(), prm_t.ap()
    w_in, w_pool, w_out, w_up, w_dn = win_t.ap(), wpool_t.ap(), wout_t.ap(), wup_t.ap(), wdn_t.ap()
    y, ncs, nps, ncp, npp = y_t.ap(), ncs_t.ap(), nps_t.ap(), ncp_t.ap(), npp_t.ap()

    with contextlib.ExitStack() as st:
        arena = st.enter_context(nc.sbuf_tensor("arena", [128, ARENA_BYTES // 4], F32))
        psum = st.enter_context(nc.psum_tensor("psum", [128, 8, 512], F32))

        PE = Eng(nc, st, nc.tensor, "pe", self_wait=False)
        ACT = Eng(nc, st, nc.scalar, "act")
        DVE = Eng(nc, st, nc.vector, "dve")
        POOL = Eng(nc, st, nc.gpsimd, "pool")
        SP = Eng(nc, st, nc.sync, "sp")

        def view(off, dtype, nelem, parts=128):
            assert off % 4 == 0
            nb = nelem * (4 if dtype == F32 else 2)
            assert nb % 4 == 0 and off + nb <= ARENA_BYTES, (off, nb)
            v = arena[0:parts, off // 4:(off + nb) // 4]
            if dtype != F32:
                v = v.bitcast(dtype)
            return v

        cur = [0]

        def alloc(nbytes):
            o = cur[0]
            cur[0] += (nbytes + 31) // 32 * 32
            return o

        o_identb = alloc(256)
        o_identf = alloc(512)
        o_onesd = alloc(256)
        o_prm = alloc(8 * 35 * 4)
        o_wp = alloc(4 * 2 * 256 * 2)
        o_gb = alloc(D * 4)
        o_stat = alloc(6 * 16 * 4)
        o_epsc = alloc(32)
        o_onec = alloc(32)
        o_negb = alloc(32)
        o_r2 = cur[0]
        o_ht = alloc(16 * NCOL * 2)
        o_win = alloc(2 * 16 * 256 * 2)
        o_xt = alloc(2 * D * 4)
        o_xn = alloc(2 * D * 2)
        o_ct = alloc(8 * NTOK * 2)
        o_dt = alloc(8 * NTOK * 2)
        o_r3a = cur[0]
        o_sig = alloc(2 * NCOL * 4)
        o_xbp = alloc(2 * PC * 4)
        o_sab = alloc(2 * (PC + NSEQ * 19) * 4)
        o_ufp = alloc(8 * 94 * 4)
        o_xbfp = alloc(8 * 94 * 4)
        assert cur[0] - o_r3a >= 2 * 16 * 512 * 2
        o_r3b = cur[0]
        o_up = alloc(8 * PC * 2)
        o_us = alloc(8 * NSEQ * 34 * 2)
        assert cur[0] - o_r3b >= 8 * NTOK * 2
        o_r4 = cur[0]
        o_xbs = alloc(8 * NSEQ * 19 * 4)
        o_ptmp = alloc(2 * NTOK * 2)
        o_outst = o_sig
        assert 2 * CC * 4 <= 2 * NCOL * 4
        o_sld = alloc(2 * CC * 4)
        o_sldb = alloc(CC * 2)
        if cur[0] - o_r4 < 3 * 16 * 256 * 2:
            alloc(3 * 16 * 256 * 2 - (cur[0] - o_r4))
        assert cur[0] - o_r4 >= 3 * 16 * 256 * 2, cur[0] - o_r4
        assert cur[0] <= ARENA_BYTES, cur[0]
        o_diag = o_xt
        o_ln = o_xt + 2 * 31 * 128 * 2
        assert o_ln + 8704 <= o_xn + 2 * D * 2

        identb = view(o_identb, BF16, 128)
        identf = view(o_identf, F32, 128)
        onesd = view(o_onesd, BF16, 128)
        PRM = view(o_prm, F32, 8 * 35).rearrange("p (j k) -> p j k", j=8)
        WP = view(o_wp, BF16, 4 * 2 * 256).rearrange("p (g k d) -> p g k d", g=4, k=2)
        GB = view(o_gb, F32, D)
        STAT = view(o_stat, F32, 6 * 16).rearrange("p (a t) -> p a t", a=6)
        EPSC = view(o_epsc, F32, 1)
        ONEC = view(o_onec, F32, 1)
        NEGB = view(o_negb, F32, 8)
        HT = view(o_ht, BF16, 16 * NCOL).rearrange("p (c t) -> p c t", c=16)
        WIN = [view(o_win + s * 16 * 256 * 2, BF16, 16 * 256).rearrange("p (c e) -> p c e", c=16) for s in range(2)] + \
              [view(o_up + s * 16 * 256 * 2, BF16, 16 * 256).rearrange("p (c e) -> p c e", c=16) for s in range(2)]
        assert 2 * 16 * 256 * 2 <= 8 * PC * 2
        NXT = 6
        XT = [view(o_xt + s * D * 4, F32, D) for s in range(2)] + [view(o_ct + s * D * 4, F32, D) for s in range(4)]
        XN = [view(o_xn + s * D * 2, BF16, D) for s in range(2)]
        x1_offs = [o_win, o_win + 8192] + [o_r3a + k * 8192 for k in range(4)] + [o_xt + k * 8192 for k in range(3)]
        X1 = [view(x1_offs[t], F32, D) for t in range(9)]
        CT = view(o_ct, BF16, 8 * NTOK).rearrange("p (c t) -> p c t", c=8)
        DT = view(o_dt, BF16, 8 * NTOK).rearrange("p (c t) -> p c t", c=8)
        H2T = view(o_ct, BF16, 16 * NTOK).rearrange("p (c t) -> p c t", c=16)
        JUNK = view(o_ct, BF16, D)
        SIG = [view(o_sig + s * NCOL * 4, F32, NCOL) for s in range(2)]
        XBP = [view(o_xbp + s * PC * 4, F32, PC) for s in range(2)]
        SAB = [view(o_sab + s * (PC + NSEQ * 19) * 4, F32, PC + NSEQ * 19) for s in range(2)]
        UFP = view(o_ufp, F32, 8 * 94).rearrange("p (j t) -> p j t", j=8)
        XBFP = view(o_xbfp, F32, 8 * 94).rearrange("p (j t) -> p j t", j=8)
        WOUT = [view(o_ht + s * 16 * 512 * 2, BF16, 16 * 512).rearrange("p (c e) -> p c e", c=16) for s in range(2)]
        WD = view(o_ht, BF16, 8 * D).rearrange("p (c e) -> p c e", c=8)
        UP = view(o_up, BF16, 8 * PC).rearrange("p (j t) -> p j t", j=8)
        US = view(o_us, BF16, 8 * NSEQ * 34).rearrange("p (j s t) -> p j s t", j=8, s=NSEQ)
        AT = view(o_r3b, BF16, 8 * NTOK).rearrange("p (c t) -> p c t", c=8)
        NXN2 = 6
        XN2 = [view(o_r3b + s * D * 2, BF16, D) for s in range(NXN2)]
        assert NXN2 * D * 2 <= 8 * PC * 2 + 8 * NSEQ * 34 * 2
        XBS = view(o_xbs, F32, 8 * NSEQ * 19).rearrange("p (j s t) -> p j s t", j=8, s=NSEQ)
        PTMP = [view(o_ptmp + s * NTOK * 2, BF16, NTOK) for s in range(2)]
        OUTST = [view(o_outst + s * CC * 4, F32, CC) for s in range(2)]
        SLD = [view(o_sld + s * CC * 4, F32, CC) for s in range(2)]
        SLDB = view(o_sldb, BF16, CC)
        PRMRAW = view(o_ufp, F32, CC)
        SLDX = [SLD[0], SLD[1], view(o_sig, F32, CC), view(o_sig + 4096, F32, CC),
                view(o_sab, F32, CC), view(o_sab + 4096, F32, CC)]
        NWU = 3
        WU = [view(o_r4 + s * 16 * 256 * 2, BF16, 16 * 256).rearrange("p (c e) -> p c e", c=16) for s in range(NWU)]
        RELU = [view(o_r3b + 8 * NTOK * 2 + s * 512 * 4, F32, 512) for s in range(3)]
        assert 8 * NTOK * 2 + 3 * 512 * 4 <= 8 * PC * 2 + 8 * NSEQ * 34 * 2
        DIAG = [view(o_diag + s * 31 * 128 * 2, BF16, 31 * 128).rearrange("p (k m) -> p k m", k=31) for s in range(2)]
        C32 = [view(o_ln + s * 2048, F32, 512) for s in range(2)] + [view(o_ptmp + s * 2048, F32, 512) for s in range(2)]
        CB = view(o_ln + 4096, BF16, 512)
        CSQ = view(o_ln + 5120, BF16, 512)
        M2 = view(o_ln + 6144, F32, 512)
        VV = view(o_sld, F32, 512)
        SG = view(o_sld + 2048, F32, 512)
        VVS = [VV, view(o_sld + 4096, F32, 512)]

        R = {}

        def res(name):
            if name not in R:
                R[name] = Res()
            return R[name]

        banks = [Res() for _ in range(8)]
        bank_ctr = [0]

        held = set()

        def next_bank(hold=False):
            while True:
                b = bank_ctr[0] % 8
                bank_ctr[0] += 1
                if b not in held:
                    break
            if hold:
                held.add(b)
            return b

        def pbank(b, parts=128, n=512):
            return psum[0:parts, b, 0:n]

        def pbank_bf(b, parts=128):
            return psum[0:parts, b, :].bitcast(BF16)

        dsems = {}

        def dsem(name):
            if name not in dsems:
                dsems[name] = DmaSem(nc, st, name)
            return dsems[name]

        r_identf, r_identb, r_onesd = res("identf"), res("identb"), res("onesd")
        r_stats = [[res("stat%d_%d" % (ph, t)) for t in range(9)] for ph in range(3)]
        r_stat_all = [r for l in r_stats for r in l]
        op(POOL, lambda: nc.gpsimd.memset(identf, 1.0), writes=[r_identf])
        op(POOL, lambda: nc.gpsimd.affine_select(out=identf, in_=identf, pattern=[[-1, 128]],
                                                 compare_op=ALU.is_ge, fill=0.0, base=0, channel_multiplier=1),
           reads=[r_identf], writes=[r_identf])
        op(POOL, lambda: nc.gpsimd.affine_select(out=identf, in_=identf, pattern=[[1, 128]],
                                                 compare_op=ALU.is_ge, fill=0.0, base=0, channel_multiplier=-1),
           reads=[r_identf], writes=[r_identf])
        op(POOL, lambda: nc.gpsimd.tensor_copy(identb, identf), reads=[r_identf], writes=[r_identb])
        op(POOL, lambda: nc.gpsimd.memset(onesd, 1.0 / 128.0), writes=[r_onesd])
        op(POOL, lambda: nc.gpsimd.memset(STAT, 0.0), writes=r_stat_all)
        op(POOL, lambda: nc.gpsimd.memset(EPSC, EPS), writes=[r_onesd])
        op(POOL, lambda: nc.gpsimd.memset(ONEC, 1.0), writes=[r_onesd])
        ACT.wait(r_onesd.w)

        r_gb = res("gb")

        def load_g(idx):
            dma(SP, dsem("gb"), GB, bass.AP(gs_t, idx * D, [[0, 128], [1, D]]), writes=[r_gb])

        r_prmraw, r_prm, r_wp = res("prmraw"), res("prm"), res("wp")
        r_xt = [res("xt%d" % i) for i in range(NXT)]
        r_xnr = [res("xn0"), res("xn1")]
        r_ht = res("ht")

        A0_ORDER = [0, 1, 2, 3, 4, 5, 6, 7, 8]

        def load_xt(pos):
            t = A0_ORDER[pos]
            nrows = 128 if t < 8 else 94
            nld = 128 if t < 8 else 96
            dma(SP, dsem("xt%d" % (pos % NXT)), XT[pos % NXT][0:nld, :], xin[128 * t:128 * t + nld, :],
                writes=[r_xt[pos % NXT]])
        load_xt(0)
        load_g(0)
        dma(SP, dsem("prmraw"), PRMRAW[0:35, :], prm_d, writes=[r_prmraw])
        for pos in range(1, NXT):
            load_xt(pos)
        r_sldx = [res("sldx%d" % i) for i in range(6)]
        for q in range(4):
            dma(SP, dsem("sldx%d" % q), SLDX[q][0:120, :], sc[4 * q:4 * q + 4].rearrange("s r c -> (s r) c"),
                writes=[r_sldx[q]])
        for q in range(2):
            dma(SP, dsem("sldx%d" % (4 + q)), SLDX[4 + q][0:120, :], sp[8 * q:8 * q + 8].rearrange("s r c -> (s r) c"),
                writes=[r_sldx[4 + q]])
        dma(POOL, dsem("wp"), WP, w_pool.rearrange("g (k p) d -> p g k d", p=128), writes=[r_wp])

        b = next_bank()

        def f_prm_tr():
            ins = None
            for j in range(8):
                ins = nc.tensor.transpose(psum[:, b, j * 35:(j + 1) * 35], PRMRAW[0:35, j * 128:(j + 1) * 128],
                                          identf[0:35, 0:35])
            return ins
        op(PE, f_prm_tr, reads=[r_prmraw, r_identf], writes=[banks[b]])
        op(ACT, lambda: nc.scalar.copy(PRM, psum[:, b, 0:280].rearrange("p (j k) -> p j k", j=8)),
           reads=[banks[b]], writes=[r_prm])
        op(DVE, lambda: nc.vector.tensor_scalar(out=NEGB, in0=PRM[:, :, 33], scalar1=-1.0, scalar2=None, op0=ALU.mult),
           reads=[r_prm], writes=[r_prm])

        r_sldb, r_xbs = res("sldb"), res("xbs")
        r_us = [res("us%d" % j) for j in range(8)]
        r_up = [res("up%d" % j) for j in range(8)]
        def conv_hist(q):
          if True:
            op(ACT, lambda: nc.scalar.copy(SLDB[0:120, :], SLDX[q][0:120, :]), reads=[r_sldx[q]], writes=[r_sldb])
            b = next_bank()
            bb = pbank_bf(b)

            def f_tr():
                ins = None
                for j in range(8):
                    ins = nc.tensor.transpose(bb[:, j * 120:(j + 1) * 120], SLDB[0:120, j * 128:(j + 1) * 128],
                                              identb[0:120, 0:120])
                return ins
            op(PE, f_tr, reads=[r_sldb, r_identb], writes=[banks[b]])
            op(ACT, lambda: nc.scalar.copy(US[:, :, 4 * q:4 * q + 4, 0:30],
                                           bb[:, 0:960].rearrange("p (j s r) -> p j s r", j=8, s=4)),
               reads=[banks[b]], writes=r_us)

        def pool_hist():
          for q in range(2):
            for hh in range(2):
                b = next_bank()

                def f_tr():
                    ins = None
                    for jj in range(4):
                        j = hh * 4 + jj
                        ins = nc.tensor.transpose(psum[:, b, jj * 120:(jj + 1) * 120],
                                                  SLDX[4 + q][0:120, j * 128:(j + 1) * 128], identf[0:120, 0:120])
                    return ins
                op(PE, f_tr, reads=[r_sldx[4 + q], r_identf], writes=[banks[b]])
                op(ACT, lambda: nc.scalar.copy(XBS[:, hh * 4:hh * 4 + 4, 8 * q:8 * q + 8, 0:15],
                                               psum[:, b, 0:480].rearrange("p (j s r) -> p j s r", j=4, s=8)),
                   reads=[banks[b]], writes=[r_xbs])

        def norm_a(t, src, nrows, phase, xn_slot, r_src, r_xn, part="both"):
            if part == "both":
                norm_a(t, src, nrows, phase, xn_slot, r_src, r_xn, "act")
                norm_a(t, src, nrows, phase, xn_slot, r_src, r_xn, "dve")
                return
            ss = STAT[0:nrows, 2 * phase, t:t + 1]
            rs = STAT[0:nrows, 2 * phase + 1, t:t + 1]
            r_stat = r_stats[phase][t]
            if part == "act":
                op(ACT, lambda: nc.scalar.activation(out=xn_slot[0:nrows, :], in_=src[0:nrows, :], func=AF.Square,
                                                     accum_out=ss), reads=[r_src], writes=[r_xn, r_stat])
                op(ACT, lambda: nc.scalar.activation(out=rs, in_=ss, func=AF.Ln, scale=1.0 / D, bias=EPSC[0:nrows, :]),
                   reads=[r_stat], writes=[r_stat])
                op(ACT, lambda: nc.scalar.activation(out=rs, in_=rs, func=AF.Exp, scale=-0.5),
                   reads=[r_stat], writes=[r_stat])
                return
            op(DVE, lambda: nc.vector.scalar_tensor_tensor(out=xn_slot[0:nrows, :], in0=src[0:nrows, :], scalar=rs,
                                                           in1=GB[0:nrows, :], op0=ALU.mult, op1=ALU.mult),
               reads=[r_src, r_stat, r_gb], writes=[r_xn])

        evac_ctr = [0]

        def norm_b(nrows, xn_slot, r_xn, evac):
            for half in range(2):
                b = next_bank()
                bb = pbank_bf(b)

                def f_tr():
                    ins = None
                    for c in range(8):
                        dc = half * 8 + c
                        ins = nc.tensor.transpose(bb[:, c * 128:c * 128 + nrows],
                                                  xn_slot[0:nrows, dc * 128:(dc + 1) * 128],
                                                  identb[0:nrows, 0:nrows])
                    return ins
                op(PE, f_tr, reads=[r_xn, r_identb], writes=[banks[b]])
                evac(b, bb, half, ACT if (half == 0 and evac_ctr[0] % 2 == 0) else DVE)
            evac_ctr[0] += 1

        def cp(eng, out, in_):
            return nc.scalar.copy(out, in_) if eng is ACT else nc.vector.tensor_copy(out, in_)

        def a0_a(pos):
            t = A0_ORDER[pos]
            nrows = 128 if t < 8 else 94
            norm_a(t, XT[pos % NXT], nrows, 0, XN[pos % 2], r_xt[pos % NXT], r_xnr[pos % 2])

        def a0_b(pos):
            t = A0_ORDER[pos]
            nrows = 128 if t < 8 else 94

            def evac(b, bb, half, eng):
                src3 = bb.rearrange("p (c t) -> p c t", c=8)
                if t < 8:
                    op(eng, lambda: cp(eng, HT[:, half * 8:half * 8 + 8, NH + 128 * t:NH + 128 * t + 128],
                                       src3[:, :, 0:128]), reads=[banks[b]], writes=[r_ht])
                else:
                    op(eng, lambda: cp(eng, HT[:, half * 8:half * 8 + 8, PC:PC + NS], src3[:, :, 0:NS]),
                       reads=[banks[b]], writes=[r_ht])
                    op(eng, lambda: cp(eng, HT[:, half * 8:half * 8 + 8, 0:NH], src3[:, :, NS:NS + NH]),
                       reads=[banks[b]], writes=[r_ht])
            norm_b(nrows, XN[pos % 2], r_xnr[pos % 2], evac)

        a0_a(0)
        for t in range(9):
            if t + 1 < 9:
                a0_a(t + 1)
            if t + NXT < 9:
                load_xt(t + NXT)
            a0_b(t)
        pool_hist()

        TB = [(0, 512), (512, 512), (1024, 94)]
        r_win = [res("win%d" % i) for i in range(4)]
        r_sig = [res("sig0"), res("sig1")]
        r_xbp = [res("xbp0"), res("xbp1")]
        r_sab = [res("sab0"), res("sab1")]
        r_ufp, r_xbfp = res("ufp"), res("xbfp")
        r_dt = [res("dt%d" % j) for j in range(8)]
        r_ct = [res("ct%d" % j) for j in range(8)]
        r_ptmp = [res("ptmp0"), res("ptmp1")]
        r_diag = [res("diag0"), res("diag1")]
        r_ln = {k: res("ln_" + k) for k in ("c32a", "c32b", "cb", "csq", "m2", "vv", "sg", "zz")}
        win_ctr = [0]

        def load_win(blk, slot=None):
            if slot is None:
                s_ = win_ctr[0] % 2
                win_ctr[0] += 1
            else:
                s_ = slot
            dma(POOL, dsem("win%d" % s_), WIN[s_],
                w_in[:, 256 * blk:256 * blk + 256].rearrange("(c p) e -> p c e", p=128), writes=[r_win[s_]])
            return s_

        def win_chunk(s_, hh, consumer):
            for tb, (c0, n) in enumerate(TB):
                b = next_bank()

                def f_mm():
                    ins = None
                    for dc in range(16):
                        ins = nc.tensor.matmul(pbank(b, 128, n), WIN[s_][:, dc, hh * 128:(hh + 1) * 128],
                                               HT[:, dc, c0:c0 + n], start=(dc == 0), stop=(dc == 15))
                    return ins
                op(PE, f_mm, reads=[r_win[s_], r_ht], writes=[banks[b]])
                consumer(tb, c0, n, b)

        def pooling(j, xs):
            w = WINDOWS[j // 2]
            steps = {2: 1, 4: 2, 8: 3, 16: 4}[w]
            X = XBP[xs]
            XS = XBS[:, j]
            srcP, srcS = X, XS
            r_src = r_xbp[xs]
            for i in range(1, steps + 1):
                sh = 2 ** (i - 1)
                lo = 2 ** i - 1
                d_ = (i - 1) % 2
                dstP = SAB[d_][:, 0:PC]
                dstS = SAB[d_][:, PC:PC + NSEQ * 19].rearrange("p (s t) -> p s t", s=NSEQ)
                rd = [r_src, r_xbs] if i == 1 else [r_src]
                op(DVE, lambda: nc.vector.tensor_tensor(out=dstP[:, lo:PC], in0=srcP[:, lo:PC], in1=srcP[:, lo - sh:PC - sh],
                                                        op=ALU.add), reads=rd, writes=[r_sab[d_]])
                op(DVE, lambda: nc.vector.tensor_tensor(out=dstS[:, :, lo:19], in0=srcS[:, :, lo:19],
                                                        in1=srcS[:, :, lo - sh:19 - sh], op=ALU.add),
                   reads=rd, writes=[r_sab[d_]])
                srcP, srcS, r_src = dstP, dstS, r_sab[d_]
            op(DVE, lambda: nc.vector.scalar_tensor_tensor(out=DT[:, j, 0:NPR], in0=srcP[:, NH:PC], scalar=1.0 / w,
                                                           in1=X[:, NH:PC], op0=ALU.mult, op1=ALU.subtract),
               reads=[r_src, r_xbp[xs]], writes=[r_dt[j]])
            op(DVE, lambda: nc.vector.scalar_tensor_tensor(
                out=DT[:, j, NPR:NTOK].rearrange("p (s t) -> p s t", s=NSEQ), in0=srcS[:, :, 15:19], scalar=1.0 / w,
                in1=XS[:, :, 15:19], op0=ALU.mult, op1=ALU.subtract),
               reads=[r_src, r_xbs, r_xbp[xs]], writes=[r_dt[j]])

        def pool_mm(gi):
            TBK = [(0, 512), (512, 512), (1024, 64)]
            for oc in range(2):
                jo = 2 * gi + oc
                for (c0, n) in TBK:
                    b = next_bank()

                    def f_mm():
                        ins = None
                        for kc in range(2):
                            ins = nc.tensor.matmul(pbank(b, 128, n), WP[:, gi, kc, oc * 128:(oc + 1) * 128],
                                                   DT[:, 2 * gi + kc, c0:c0 + n], start=(kc == 0), stop=(kc == 1))
                        return ins
                    op(PE, f_mm, reads=[r_wp, r_dt[2 * gi], r_dt[2 * gi + 1]], writes=[banks[b]])
                    if oc == 0:
                        op(ACT, lambda: nc.scalar.activation(out=PTMP[gi % 2][:, c0:c0 + n], in_=pbank(b, 128, n),
                                                             func=AF.Identity, scale=PRM[:, jo, 34:35]),
                           reads=[banks[b], r_prm], writes=[r_ptmp[gi % 2]])
                    else:
                        op(ACT, lambda: nc.scalar.activation(out=DT[:, jo, c0:c0 + n], in_=pbank(b, 128, n),
                                                             func=AF.Identity, scale=PRM[:, jo, 34:35]),
                           reads=[banks[b], r_prm], writes=[r_dt[jo]])
            op(DVE, lambda: nc.vector.tensor_copy(DT[:, 2 * gi, :], PTMP[gi % 2]), reads=[r_ptmp[gi % 2]],
               writes=[r_dt[2 * gi]])

        def build_diag(j):
            op(DVE, lambda: nc.vector.tensor_tensor(out=DIAG[j % 2], in0=bc(identb, [[0, 31], [1, 128]]),
                                                    in1=bc(PRM[:, j, 0:31], [[1, 31], [0, 128]]), op=ALU.mult),
               reads=[r_identb, r_prm], writes=[r_diag[j % 2]], free=[r_xt[0], r_xt[1], r_xnr[0], r_xnr[1]])

        def conv_chunk(j):
            ds_ = j % 2
            if j + 1 < 8:
                build_diag(j + 1)
            TBK = [(0, 512), (512, 512), (1024, 64)]
            units = []
            for tb, (c0, n) in enumerate(TBK):
                b = next_bank(hold=True)

                def f_mm():
                    ins = None
                    for k in range(31):
                        if tb < 2:
                            rhs = UP[:, j, c0 + k:c0 + k + n]
                            out = pbank(b, 128, n)
                        else:
                            rhs = US[:, j, :, k:k + 4]
                            out = pbank(b, 128, n).rearrange("p (s t) -> p s t", s=NSEQ)
                        ins = nc.tensor.matmul(out, DIAG[ds_][:, k, :], rhs, start=(k == 0), stop=(k == 30))
                    return ins
                op(PE, f_mm, reads=[r_diag[ds_], r_up[j], r_us[j]], writes=[banks[b]])
                ln_push((j, (tb, c0, n, b)))

        ln_ctr = [0]
        r_c32 = [res("c32_%d" % i) for i in range(4)]
        r_vv = [res("vv0"), res("vv1")]

        def ln_s1(item):
            j, tb, c0, n, b = item
            k = ln_ctr[0]
            ln_ctr[0] += 1
            c32 = C32[k % 4][:, 0:n]
            rc = r_c32[k % 4]
            bias = PRM[:, j, 31:32]
            fr = [r_ptmp[0], r_ptmp[1]] if k % 4 >= 2 else []
            op(ACT, lambda: nc.scalar.activation(out=c32, in_=pbank(b, 128, n), func=AF.Identity, bias=bias),
               reads=[banks[b], r_prm], writes=[rc], free=fr)
            op(DVE, lambda: nc.vector.tensor_copy(CB[:, 0:n], c32), reads=[rc], writes=[r_ln["cb"]])
            op(ACT, lambda: nc.scalar.activation(out=CSQ[:, 0:n], in_=pbank(b, 128, n), func=AF.Square, bias=bias),
               reads=[banks[b], r_prm], writes=[r_ln["csq"]])
            held.discard(b)
            bm, be = next_bank(hold=True), next_bank(hold=True)
            op(PE, lambda: nc.tensor.matmul(pbank(bm, 128, n), onesd, CB[:, 0:n], start=True, stop=True),
               reads=[r_onesd, r_ln["cb"]], writes=[banks[bm]])
            op(PE, lambda: nc.tensor.matmul(pbank(be, 128, n), onesd, CSQ[:, 0:n], start=True, stop=True),
               reads=[r_onesd, r_ln["csq"]], writes=[banks[be]])
            return (j, c0, n, k, bm, be)

        def ln_s2(item):
            j, c0, n, k, bm, be = item
            c32, rc = C32[k % 4][:, 0:n], r_c32[k % 4]
            vv, rv = VVS[k % 2][:, 0:n], r_vv[k % 2]
            op(ACT, lambda: nc.scalar.activation(out=M2[:, 0:n], in_=pbank(bm, 128, n), func=AF.Square),
               reads=[banks[bm]], writes=[r_ln["m2"]])
            op(DVE, lambda: nc.vector.tensor_tensor(out=vv, in0=pbank(be, 128, n), in1=M2[:, 0:n], op=ALU.subtract),
               reads=[banks[be], r_ln["m2"]], writes=[rv])
            op(DVE, lambda: nc.vector.tensor_tensor(out=c32, in0=c32, in1=pbank(bm, 128, n), op=ALU.subtract),
               reads=[rc, banks[bm]], writes=[rc])
            held.discard(bm)
            held.discard(be)
            return (j, c0, n, k)

        def ln_s3(item):
            j, c0, n, k = item
            c32, rc = C32[k % 4][:, 0:n], r_c32[k % 4]
            vv, rv = VVS[k % 2][:, 0:n], r_vv[k % 2]
            op(ACT, lambda: nc.scalar.activation(out=vv, in_=vv, func=AF.Ln, bias=EPSC), reads=[rv], writes=[rv])
            op(ACT, lambda: nc.scalar.activation(out=vv, in_=vv, func=AF.Exp, scale=-0.5), reads=[rv], writes=[rv])
            op(DVE, lambda: nc.vector.scalar_tensor_tensor(out=c32, in0=c32, scalar=PRM[:, j, 32:33], in1=vv,
                                                           op0=ALU.mult, op1=ALU.mult),
               reads=[rc, rv, r_prm], writes=[rc])
            return (j, c0, n, k)

        def ln_s4(item):
            j, c0, n, k = item
            c32, rc = C32[k % 4][:, 0:n], r_c32[k % 4]
            lb = PRM[:, j, 33:34]
            sg = SG[:, 0:n]
            op(ACT, lambda: nc.scalar.activation(out=sg, in_=c32, func=AF.Exp, scale=-1.0, bias=NEGB[:, j:j + 1]),
               reads=[rc, r_prm], writes=[r_ln["sg"]])
            op(ACT, lambda: nc.scalar.activation(out=sg, in_=sg, func=AF.Ln, bias=ONEC), reads=[r_ln["sg"]],
               writes=[r_ln["sg"]])
            op(ACT, lambda: nc.scalar.activation(out=sg, in_=sg, func=AF.Exp, scale=-1.0), reads=[r_ln["sg"]],
               writes=[r_ln["sg"]])
            op(DVE, lambda: nc.vector.scalar_tensor_tensor(out=CT[:, j, c0:c0 + n], in0=c32, scalar=lb, in1=SG[:, 0:n],
                                                           op0=ALU.add, op1=ALU.mult),
               reads=[rc, r_ln["sg"], r_prm], writes=[r_ct[j]])

        lq = [[], [], [], []]

        def ln_step(lag):
            if lq[3]:
                ln_s4(lq[3].pop(0))
            if lq[2]:
                lq[3].append(ln_s3(lq[2].pop(0)))
            if lq[1]:
                lq[2].append(ln_s2(lq[1].pop(0)))
            if len(lq[0]) > lag:
                lq[1].append(ln_s1(lq[0].pop(0)))

        def ln_push(item):
            j, (tb, c0, n, b) = item
            lq[0].append((j, tb, c0, n, b))
            ln_step(1)

        def ln_drain():
            while any(lq):
                ln_step(0)

        build_diag(0)
        xb_slots = [load_win(8), load_win(9), load_win(10, slot=2), load_win(11, slot=3)]
        for blk in range(8, 12):
            s_ = xb_slots[blk - 8]
            for hh in range(2):
                j = (blk - 8) * 2 + hh
                xs = j % 2

                def cons_xb(tb, c0, n, b, j=j, xs=xs):
                    if tb < 2:
                        op(ACT, lambda: nc.scalar.copy(XBP[xs][:, c0:c0 + n], pbank(b, 128, n)),
                           reads=[banks[b]], writes=[r_xbp[xs]])
                    else:
                        op(ACT, lambda: nc.scalar.copy(XBFP[:, j, :], pbank(b, 128, 94)), reads=[banks[b]],
                           writes=[r_xbfp])
                        op(ACT, lambda: nc.scalar.copy(XBP[xs][:, 1024:PC], pbank(b, 128, 30)), reads=[banks[b]],
                           writes=[r_xbp[xs]])
                        op(ACT, lambda: nc.scalar.copy(XBS[:, j, :, 15:19],
                                                       psum[:, b, 30:94].rearrange("p (s t) -> p s t", s=NSEQ)),
                           reads=[banks[b]], writes=[r_xbs])
                win_chunk(s_, hh, cons_xb)
                if j < 4:
                    conv_hist(j)
                pooling(j, xs)
        r_x1 = [res("x1_%d" % t) for t in range(9)]
        r_wout = [res("wout0"), res("wout1")]
        TT = [(128 * t, 128) for t in range(8)] + [(1024, 64)]

        def load_wout(nb):
            s_ = nb % 2
            dma(POOL, dsem("wout%d" % s_), WOUT[s_],
                w_out[:, 512 * nb:512 * nb + 512].rearrange("(c p) e -> p c e", p=128),
                writes=[r_wout[s_]], free=[r_ht])

        def load_x1(t, free):
            r0, m_ = TT[t]
            dma(SP, dsem("x1_%d" % t), X1[t][0:m_, :], xin[r0:r0 + m_, :], writes=[r_x1[t]], free=free)

        conv_units = {}
        ln1 = {}

        def do_conv(jlist):
            for j in jlist:
                conv_chunk(j)

        for m in range(4):
            s_g = load_win(4 + m)
            for hh in range(2):
                j = 2 * m + hh

                def cons_gate(tb, c0, n, b, j=j):
                    sg = SIG[j % 2][:, c0:c0 + n]
                    op(ACT, lambda: nc.scalar.activation(out=sg, in_=pbank(b, 128, n), func=AF.Exp, scale=-1.0),
                       reads=[banks[b]], writes=[r_sig[j % 2]])
                    op(ACT, lambda: nc.scalar.activation(out=sg, in_=sg, func=AF.Ln, bias=ONEC), reads=[r_sig[j % 2]],
                       writes=[r_sig[j % 2]])
                    op(ACT, lambda: nc.scalar.activation(out=sg, in_=sg, func=AF.Exp, scale=-1.0), reads=[r_sig[j % 2]],
                       writes=[r_sig[j % 2]])
                win_chunk(s_g, hh, cons_gate)
            s_a = load_win(m)
            for hh in range(2):
                j = 2 * m + hh

                def cons_a(tb, c0, n, b, j=j):
                    if tb < 2:
                        op(DVE, lambda: nc.vector.tensor_tensor(out=UP[:, j, c0:c0 + n], in0=pbank(b, 128, n),
                                                                in1=SIG[j % 2][:, c0:c0 + n], op=ALU.mult),
                           reads=[banks[b], r_sig[j % 2]], writes=[r_up[j]])
                    else:
                        op(DVE, lambda: nc.vector.tensor_tensor(out=UFP[:, j, :], in0=pbank(b, 128, 94),
                                                                in1=SIG[j % 2][:, 1024:NCOL], op=ALU.mult),
                           reads=[banks[b], r_sig[j % 2]], writes=[r_ufp])
                        op(DVE, lambda: nc.vector.tensor_copy(UP[:, j, 1024:PC], UFP[:, j, 0:30]), reads=[r_ufp],
                           writes=[r_up[j]])
                        op(DVE, lambda: nc.vector.tensor_copy(US[:, j, :, 30:34],
                                                              UFP[:, j, 30:94].rearrange("p (s t) -> p s t", s=NSEQ)),
                           reads=[r_ufp], writes=[r_us[j]])
                win_chunk(s_a, hh, cons_a)
            if m == 0:
                pool_mm(0)
                pool_mm(1)
            if m == 1:
                pool_mm(2)
                pool_mm(3)
            if m == 3:
                load_wout(0)
                load_wout(1)
                load_x1(0, [r_win[0], r_win[1]])
                load_x1(1, [r_win[0], r_win[1]])
            if m >= 1:
                do_conv([2 * m - 1])
            do_conv([2 * m])

        def state_out(SRC, r_src, slot, prompt_rows, dst_p, dst_s, s_off, name):
            ob = [next_bank(), next_bank()]
            for hh in range(2):
                b = ob[hh]

                def f_tr():
                    ins = None
                    for jj in range(4):
                        ins = nc.tensor.transpose(psum[0:94, b, jj * 128:(jj + 1) * 128], SRC[:, hh * 4 + jj, :], identf)
                    return ins
                op(PE, f_tr, reads=[r_src, r_identf], writes=[banks[b]])
                op(ACT, lambda: nc.scalar.copy(OUTST[slot][0:94, hh * 512:(hh + 1) * 512], psum[0:94, b, :]),
                   reads=[banks[b]], writes=[res("outst%d" % slot)], free=[r_sig[0], r_sig[1]])
            r_o = res("outst%d" % slot)
            lo = 30 - prompt_rows
            dma(SP, dsem(name), dst_p, OUTST[slot][lo:30, :], reads=[r_o])
            for s in range(NSEQ):
                dma(SP, dsem(name), dst_s[s, s_off:s_off + 4, :], OUTST[slot][30 + 4 * s:34 + 4 * s, :], reads=[r_o])

        state_out(UFP, r_ufp, 0, 30, ncp, ncs, 26, "st_u")
        state_out(XBFP, r_xbfp, 1, 15, npp, nps, 11, "st_x")
        r3a_users = [r_sig[0], r_sig[1], r_xbp[0], r_xbp[1], r_sab[0], r_sab[1], r_ufp, r_xbfp,
                     res("outst0"), res("outst1"), r_prmraw] + r_sldx
        for t in range(2, 6):
            load_x1(t, r3a_users)
        do_conv([7])
        ln_step(0)
        NPRE = 5
        PRE_EC = [8 + i for i in range(8)] + [0, 1, 2, 3, 4, 5]
        pre_banks = {}

        def cp_chunk(ec):
            return (CT[:, ec], r_ct[ec]) if ec < 8 else (DT[:, ec - 8], r_dt[ec - 8])

        for t in range(NPRE):
            r0, m_ = TT[t]
            b = next_bank(hold=True)
            pre_banks[t] = b

            def f_mm():
                ins = None
                for i, ec in enumerate(PRE_EC):
                    a_, _ = cp_chunk(ec)
                    ins = nc.tensor.matmul(pbank(b, m_, 512), a_[:, r0:r0 + m_], WOUT[0][:, ec, :],
                                           start=(i == 0), stop=False)
                return ins
            op(PE, f_mm, reads=[r_wout[0]] + [r_ct[e] for e in range(6)] + r_dt, writes=[banks[b]])
        ln_drain()
        xtxn_users = [r_xt[0], r_xt[1], r_xnr[0], r_xnr[1], r_diag[0], r_diag[1]] + \
            [r_ln[k] for k in ("cb", "csq", "m2")] + [r_c32[0], r_c32[1]]
        for t in range(6, 9):
            load_x1(t, xtxn_users)
        dma(SP, dsem("hist"), ncs[:, 0:26, :], sc[:, 4:30, :])
        dma(SP, dsem("hist"), nps[:, 0:11, :], sp[:, 4:15, :])

        if debug:
            dma(SP, dsem("dbg"), dbg_ct, view(o_ct, BF16, 8 * NTOK), reads=r_ct)
            dma(SP, dsem("dbg"), dbg_dt, view(o_dt, BF16, 8 * NTOK), reads=r_dt)
        r_h2t = res("h2t")
        r_xn2 = [res("xn2_%d" % i) for i in range(NXN2)]
        r1_users = r_ct + r_dt
        r3b_users = r_up + r_us
        load_g(1)

        def d0_a(t, part="both"):
            r0, m_ = TT[t]
            norm_a(t, X1[t], m_, 1, XN2[t % NXN2], r_x1[t], r_xn2[t % NXN2], part)

        def d0_b(t):
            r0, m_ = TT[t]

            def evac(b, bb, half, eng):
                src3 = bb.rearrange("p (c t) -> p c t", c=8)
                op(eng, lambda: cp(eng, H2T[:, half * 8:half * 8 + 8, r0:r0 + m_], src3[:, :, 0:m_]),
                   reads=[banks[b]], writes=[r_h2t], free=r1_users)
            norm_b(m_, XN2[t % NXN2], r_xn2[t % NXN2], evac)

        for nb in range(4):
            s_ = nb % 2
            for t in range(9):
                r0, m_ = TT[t]
                pre = (nb == 0 and t in pre_banks)
                b = pre_banks[t] if pre else next_bank()
                ecs = [6, 7] if pre else list(range(16))

                def f_mm():
                    ins = None
                    for i, ec in enumerate(ecs):
                        a_, _ = cp_chunk(ec)
                        ins = nc.tensor.matmul(pbank(b, m_, 512), a_[:, r0:r0 + m_], WOUT[s_][:, ec, :],
                                               start=(i == 0 and not pre), stop=(i == len(ecs) - 1))
                    return ins
                op(PE, f_mm, reads=[r_wout[s_]] + r_ct + r_dt, writes=[banks[b]])
                xs_ = X1[t][0:m_, nb * 512:(nb + 1) * 512]
                op(DVE, lambda: nc.vector.tensor_tensor(out=xs_, in0=pbank(b, m_, 512), in1=xs_, op=ALU.add),
                   reads=[banks[b]], writes=[r_x1[t]])
                held.discard(b)
                if nb == 3 and t < NXN2:
                    if t == 0:
                        for rr in r3b_users:
                            DVE.wait(rr.w, *rr.r.values())
                            ACT.wait(rr.w, *rr.r.values())
                    d0_a(t, "act")
                if nb == 3 and 1 <= t <= NXN2:
                    d0_a(t - 1, "dve")
            if nb + 2 < 4:
                load_wout(nb + 2)

        if debug:
            for t in range(9):
                dma(SP, dsem("dbg"), dbg_x1[t], X1[t], reads=[r_x1[t]])
        for t in range(9):
            d0_b(t)
            if t + NXN2 < 9:
                d0_a(t + NXN2)

        r_wu = [res("wu%d" % i) for i in range(NWU)]
        r_wd = res("wd")
        r_at = [res("at%d" % i) for i in range(8)]
        r_relu = [res("relu%d" % i) for i in range(3)]
        r4_users = [r_xbs, r_ptmp[0], r_ptmp[1], r_sldx[0], r_sldx[1], r_sldb,
                    r_vv[0], r_vv[1], r_ln["sg"], r_c32[2], r_c32[3]]
        wu_ctr = [0]
        relu_ctr = [0]
        TBM = [(0, 363), (363, 363), (726, 362)]

        def load_wu(blk):
            s_ = blk % NWU
            dma(POOL, dsem("wu%d" % s_), WU[s_],
                w_up[:, 256 * blk:256 * blk + 256].rearrange("(c p) e -> p c e", p=128),
                writes=[r_wu[s_]], free=r4_users)

        def load_wd(g):
            dma(POOL, dsem("wd"), WD, w_dn[1024 * g:1024 * g + 1024, :].rearrange("(c p) e -> p c e", p=128),
                writes=[r_wd], free=[r_wout[0], r_wout[1], r_ht])

        for blk in range(NWU):
            load_wu(blk)
        load_g(2)
        def final_scale_store(t):
            r0, m_ = TT[t]
            rs = STAT[0:m_, 5, t:t + 1]
            op(DVE, lambda: nc.vector.scalar_tensor_tensor(out=X1[t][0:m_, :], in0=X1[t][0:m_, :], scalar=rs,
                                                           in1=GB[0:m_, :], op0=ALU.mult, op1=ALU.mult),
               reads=[r_stats[2][t], r_gb], writes=[r_x1[t]])
            dma(SP, dsem("yout"), y[r0:r0 + m_, :], X1[t][0:m_, :], reads=[r_x1[t]])

        for g in range(8):
            for bi in range(4):
                blk = 4 * g + bi
                s_ = blk % NWU
                for hh in range(2):
                    fcl = bi * 2 + hh
                    for (c0, n) in TBM:
                        b = next_bank()

                        def f_mm():
                            ins = None
                            for dc in range(16):
                                ins = nc.tensor.matmul(pbank(b, 128, n), WU[s_][:, dc, hh * 128:(hh + 1) * 128],
                                                       H2T[:, dc, c0:c0 + n], start=(dc == 0), stop=(dc == 15))
                            return ins
                        op(PE, f_mm, reads=[r_wu[s_], r_h2t], writes=[banks[b]])
                        rs_ = relu_ctr[0] % 3
                        relu_ctr[0] += 1
                        op(ACT, lambda: nc.scalar.activation(out=RELU[rs_][:, 0:n], in_=pbank(b, 128, n), func=AF.Relu),
                           reads=[banks[b]], writes=[r_relu[rs_]], free=r3b_users + r_xn2)
                        op(DVE, lambda: nc.vector.tensor_tensor(out=AT[:, fcl, c0:c0 + n], in0=RELU[rs_][:, 0:n],
                                                                in1=RELU[rs_][:, 0:n], op=ALU.mult),
                           reads=[r_relu[rs_]], writes=[r_at[fcl]], free=r3b_users + r_xn2)
                if bi == 0:
                    load_wd(g)
                if blk + NWU < 32:
                    load_wu(blk + NWU)
            for t in range(9):
                r0, m_ = TT[t]
                for nb in range(4):
                    b = next_bank()

                    def f_mm():
                        ins = None
                        for fc in range(8):
                            PE.wait(r_at[fc].w)
                            ins = nc.tensor.matmul(pbank(b, m_, 512), AT[:, fc, r0:r0 + m_],
                                                   WD[:, fc, nb * 512:(nb + 1) * 512], start=(fc == 0), stop=(fc == 7))
                        return ins
                    tk = op(PE, f_mm, reads=[r_wd], writes=[banks[b]])
                    for fc in range(8):
                        r_at[fc].r[tk[0]] = tk
                    xs_ = X1[t][0:m_, nb * 512:(nb + 1) * 512]
                    op(DVE, lambda: nc.vector.tensor_tensor(out=xs_, in0=pbank(b, m_, 512), in1=xs_, op=ALU.add),
                       reads=[banks[b]], writes=[r_x1[t]])
                if g == 7:
                    ss = STAT[0:m_, 4, t:t + 1]
                    rs = STAT[0:m_, 5, t:t + 1]
                    r_stat = r_stats[2][t]
                    op(ACT, lambda: nc.scalar.activation(out=JUNK[0:m_, :], in_=X1[t][0:m_, :], func=AF.Square,
                                                         accum_out=ss),
                       reads=[r_x1[t]], writes=[r_h2t, r_stat])
                    op(ACT, lambda: nc.scalar.activation(out=rs, in_=ss, func=AF.Ln, scale=1.0 / D,
                                                         bias=EPSC[0:m_, :]), reads=[r_stat], writes=[r_stat])
                    op(ACT, lambda: nc.scalar.activation(out=rs, in_=rs, func=AF.Exp, scale=-0.5),
                       reads=[r_stat], writes=[r_stat])
                    if t >= 1:
                        final_scale_store(t - 1)
        final_scale_store(8)

        for name in ("yout", "st_u", "st_x", "hist") + (("dbg",) if debug else ()):
            d_ = dsems[name]
            SP.h.wait_ge(d_.sem, d_.n)
    return nc


_NC_CACHE = {}


def kernel(x_prompt, x_sample, state_conv, state_pool, meta_tokens, norm_mix_g, w_in, w_dw, b_dw,
           conv_ln_g, conv_ln_b, w_pool, pool_scale, w_out, norm_ffn_g, w_up, w_down, final_norm_g):
    f = lambda a: np.ascontiguousarray(np.asarray(a, dtype=np.float32))
    x_prompt, x_sample, state_conv, state_pool, meta_tokens = map(f, (x_prompt, x_sample, state_conv, state_pool, meta_tokens))
    gs = f(np.stack([np.asarray(norm_mix_g)[0], np.asarray(norm_ffn_g)[0], np.asarray(final_norm_g)], axis=0))
    prm = f(np.concatenate([np.asarray(w_dw)[0], np.asarray(b_dw), np.asarray(conv_ln_g), np.asarray(conv_ln_b),
                            np.asarray(pool_scale)], axis=0))
    w_in_, w_pool_, w_out_, w_up_, w_down_ = f(w_in)[0], f(w_pool)[0], f(w_out)[0], f(w_up)[0], f(w_down)[0]
    in_maps = []
    for i in range(8):
        b, h = i // 2, i % 2
        if h == 0:
            halo = np.concatenate([np.zeros((NH - 16, D), np.float32), meta_tokens], axis=0)
        else:
            halo = x_prompt[b, NPR - NH:NPR]
        xin = np.concatenate([x_prompt[b, h * NPR:(h + 1) * NPR], x_sample[16 * i:16 * i + 16].reshape(NS, D), halo,
                              np.zeros((2, D), np.float32)], axis=0)
        in_maps.append({
            "xin": np.ascontiguousarray(xin), "sc": np.ascontiguousarray(state_conv[0, 16 * i:16 * i + 16]),
            "sp": np.ascontiguousarray(state_pool[0, 16 * i:16 * i + 16]), "gs": gs, "prm": prm,
            "w_in": w_in_, "w_pool": w_pool_, "w_out": w_out_, "w_up": w_up_, "w_down": w_down_,
        })
    if "nc" not in _NC_CACHE:
        _NC_CACHE["nc"] = build()
    res = run_bass_kernel_spmd(_NC_CACHE["nc"], in_maps, core_ids=list(range(8)))
    outs = res.results
    y_prompt = np.zeros((4, 2048, D), np.float32)
    y_sample = np.zeros((128, 4, D), np.float32)
    ncp = np.zeros((1, 4, 30, CC), np.float32)
    npp = np.zeros((1, 4, 15, CC), np.float32)
    ncs = np.zeros((1, 128, 30, CC), np.float32)
    nps = np.zeros((1, 128, 15, CC), np.float32)
    for i in range(8):
        b, h = i // 2, i % 2
        o = outs[i]
        y_prompt[b, h * NPR:(h + 1) * NPR] = o["y"][0:NPR]
        y_sample[16 * i:16 * i + 16] = o["y"][NPR:NTOK].reshape(16, 4, D)
        ncs[0, 16 * i:16 * i + 16] = o["ncs"]
        nps[0, 16 * i:16 * i + 16] = o["nps"]
        if h == 1:
            ncp[0, b] = o["ncp"]
            npp[0, b] = o["npp"]
    return (y_prompt, y_sample, ncp, npp, ncs, nps)
```

```python
import contextlib
import numpy as np
import concourse.bass as bass
import concourse.mybir as mybir
from concourse.bass_utils import run_bass_kernel_spmd

F32 = mybir.dt.float32
BF16 = mybir.dt.bfloat16
AF = mybir.ActivationFunctionType
ALU = mybir.AluOpType

D = 2048
NPR = 1024
NH = 30
NS = 64
NSEQ = 16
PC = NH + NPR
NCOL = PC + NS
NTOK = NPR + NS
CC = 1024
DFF = 8192
EPS = 1e-6
WINDOWS = (2, 4, 8, 16)
ARENA_BYTES = 212000


class Eng:
    def __init__(self, nc, stack, handle, name, self_wait=True):
        self.h = handle
        self.key = name
        self.sem = stack.enter_context(nc.semaphore("sem_" + name))
        self.n = 0
        self.seen = {}
        self.self_wait = self_wait

    def wait(self, *toks):
        for t in toks:
            if t is None:
                continue
            key, sem, val = t
            if key == self.key and not self.self_wait:
                continue
            if self.seen.get(key, 0) >= val:
                continue
            self.h.wait_ge(sem, val)
            self.seen[key] = val

    def done(self, ins):
        ins.then_inc(self.sem, 1)
        self.n += 1
        return (self.key, self.sem, self.n)


class DmaSem:
    def __init__(self, nc, stack, name):
        self.key = "dma_" + name
        self.sem = stack.enter_context(nc.semaphore("dsem_" + name))
        self.n = 0

    def done(self, ins):
        ins.then_inc(self.sem, 16)
        self.n += 16
        return (self.key, self.sem, self.n)


class Res:
    def __init__(self):
        self.w = None
        self.r = {}


def _pre(eng, reads, writes, free=()):
    for r in reads:
        eng.wait(r.w)
    for r in list(writes) + list(free):
        eng.wait(r.w)
        eng.wait(*r.r.values())


def _post(tok, reads, writes):
    for r in reads:
        r.r[tok[0]] = tok
    for r in writes:
        r.w = tok
        r.r = {}


def op(eng, fn, reads=(), writes=(), free=()):
    _pre(eng, reads, writes, free)
    ins = fn()
    tok = eng.done(ins)
    _post(tok, reads, writes)
    return tok


def dma(queue, dsem, out, in_, reads=(), writes=(), free=()):
    _pre(queue, reads, writes, free)
    ins = queue.h.dma_start(out=out, in_=in_)
    tok = dsem.done(ins)
    _post(tok, reads, writes)
    return tok


def bc(ap, dims):
    return bass.AP(ap.tensor, ap.offset, [list(ap.ap[0])] + [list(d) for d in dims])


def build(debug=False):
    nc = bass.Bass("TRN2", target_bir_lowering=False)
    dr = lambda n, s, k="ExternalInput": nc.dram_tensor(n, s, F32, kind=k)
    xin_t = dr("xin", [NCOL + 2, D])
    sc_t = dr("sc", [NSEQ, 30, CC])
    sp_t = dr("sp", [NSEQ, 15, CC])
    gs_t = dr("gs", [3, D])
    prm_t = dr("prm", [35, CC])
    win_t = dr("w_in", [D, 3 * CC])
    wpool_t = dr("w_pool", [4, 256, 256])
    wout_t = dr("w_out", [D, D])
    wup_t = dr("w_up", [D, DFF])
    wdn_t = dr("w_down", [DFF, D])
    y_t = dr("y", [NTOK, D], "ExternalOutput")
    ncs_t = dr("ncs", [NSEQ, 30, CC], "ExternalOutput")
    nps_t = dr("nps", [NSEQ, 15, CC], "ExternalOutput")
    ncp_t = dr("ncp", [30, CC], "ExternalOutput")
    npp_t = dr("npp", [15, CC], "ExternalOutput")
    if debug:
        dbg_ct = nc.dram_tensor("dbg_ct", [128, 8 * NTOK], BF16, kind="ExternalOutput").ap()
        dbg_dt = nc.dram_tensor("dbg_dt", [128, 8 * NTOK], BF16, kind="ExternalOutput").ap()
        dbg_x1 = nc.dram_tensor("dbg_x1", [9, 128, D], F32, kind="ExternalOutput").ap()
    xin, sc, sp, prm_d = xin_t.ap(), sc_t.ap(), sp_t.ap(), prm_t.ap()
    w_in, w_pool, w_out, w_up, w_dn = win_t.ap(), wpool_t.ap(), wout_t.ap(), wup_t.ap(), wdn_t.ap()
    y, ncs, nps, ncp, npp = y_t.ap(), ncs_t.ap(), nps_t.ap(), ncp_t.ap(), npp_t.ap()

    with contextlib.ExitStack() as st:
        arena = st.enter_context(nc.sbuf_tensor("arena", [128, ARENA_BYTES // 4], F32))
        psum = st.enter_context(nc.psum_tensor("psum", [128, 8, 512], F32))

        PE = Eng(nc, st, nc.tensor, "pe", self_wait=False)
        ACT = Eng(nc, st, nc.scalar, "act")
        DVE = Eng(nc, st, nc.vector, "dve")
        POOL = Eng(nc, st, nc.gpsimd, "pool")
        SP = Eng(nc, st, nc.sync, "sp")

        def view(off, dtype, nelem, parts=128):
            assert off % 4 == 0
            nb = nelem * (4 if dtype == F32 else 2)
            assert nb % 4 == 0 and off + nb <= ARENA_BYTES, (off, nb)
            v = arena[0:parts, off // 4:(off + nb) // 4]
            if dtype != F32:
                v = v.bitcast(dtype)
            return v

        cur = [0]

        def alloc(nbytes):
            o = cur[0]
            cur[0] += (nbytes + 31) // 32 * 32
            return o

        o_identb = alloc(256)
        o_identf = alloc(512)
        o_onesd = alloc(256)
        o_prm = alloc(8 * 35 * 4)
        o_wp = alloc(4 * 2 * 256 * 2)
        o_gb = alloc(D * 4)
        o_stat = alloc(6 * 16 * 4)
        o_epsc = alloc(32)
        o_onec = alloc(32)
        o_negb = alloc(32)
        o_r2 = cur[0]
        o_ht = alloc(16 * NCOL * 2)
        o_win = alloc(2 * 16 * 256 * 2)
        o_xt = alloc(2 * D * 4)
        o_xn = alloc(2 * D * 2)
        o_ct = alloc(8 * NTOK * 2)
        o_dt = alloc(8 * NTOK * 2)
        o_r3a = cur[0]
        o_sig = alloc(2 * NCOL * 4)
        o_xbp = alloc(2 * PC * 4)
        o_sab = alloc(2 * (PC + NSEQ * 19) * 4)
        o_ufp = alloc(8 * 94 * 4)
        o_xbfp = alloc(8 * 94 * 4)
        assert cur[0] - o_r3a >= 2 * 16 * 512 * 2
        o_r3b = cur[0]
        o_up = alloc(8 * PC * 2)
        o_us = alloc(8 * NSEQ * 34 * 2)
        assert cur[0] - o_r3b >= 8 * NTOK * 2
        o_r4 = cur[0]
        o_xbs = alloc(8 * NSEQ * 19 * 4)
        o_ptmp = alloc(2 * NTOK * 2)
        o_outst = o_sig
        assert 2 * CC * 4 <= 2 * NCOL * 4
        o_sld = alloc(2 * CC * 4)
        o_sldb = alloc(CC * 2)
        if cur[0] - o_r4 < 3 * 16 * 256 * 2:
            alloc(3 * 16 * 256 * 2 - (cur[0] - o_r4))
        assert cur[0] - o_r4 >= 3 * 16 * 256 * 2, cur[0] - o_r4
        assert cur[0] <= ARENA_BYTES, cur[0]
        o_diag = o_xt
        o_ln = o_xt + 2 * 31 * 128 * 2
        assert o_ln + 8704 <= o_xn + 2 * D * 2

        identb = view(o_identb, BF16, 128)
        identf = view(o_identf, F32, 128)
        onesd = view(o_onesd, BF16, 128)
        PRM = view(o_prm, F32, 8 * 35).rearrange("p (j k) -> p j k", j=8)
        WP = view(o_wp, BF16, 4 * 2 * 256).rearrange("p (g k d) -> p g k d", g=4, k=2)
        GB = view(o_gb, F32, D)
        STAT = view(o_stat, F32, 6 * 16).rearrange("p (a t) -> p a t", a=6)
        EPSC = view(o_epsc, F32, 1)
        ONEC = view(o_onec, F32, 1)
        NEGB = view(o_negb, F32, 8)
        HT = view(o_ht, BF16, 16 * NCOL).rearrange("p (c t) -> p c t", c=16)
        WIN = [view(o_win + s * 16 * 256 * 2, BF16, 16 * 256).rearrange("p (c e) -> p c e", c=16) for s in range(2)] + \
              [view(o_up + s * 16 * 256 * 2, BF16, 16 * 256).rearrange("p (c e) -> p c e", c=16) for s in range(2)]
        assert 2 * 16 * 256 * 2 <= 8 * PC * 2
        NXT = 6
        XT = [view(o_xt + s * D * 4, F32, D) for s in range(2)] + [view(o_ct + s * D * 4, F32, D) for s in range(4)]
        XN = [view(o_xn + s * D * 2, BF16, D) for s in range(2)]
        x1_offs = [o_win, o_win + 8192] + [o_r3a + k * 8192 for k in range(4)] + [o_xt + k * 8192 for k in range(3)]
        X1 = [view(x1_offs[t], F32, D) for t in range(9)]
        CT = view(o_ct, BF16, 8 * NTOK).rearrange("p (c t) -> p c t", c=8)
        DT = view(o_dt, BF16, 8 * NTOK).rearrange("p (c t) -> p c t", c=8)
        H2T = view(o_ct, BF16, 16 * NTOK).rearrange("p (c t) -> p c t", c=16)
        JUNK = view(o_ct, BF16, D)
        SIG = [view(o_sig + s * NCOL * 4, F32, NCOL) for s in range(2)]
        XBP = [view(o_xbp + s * PC * 4, F32, PC) for s in range(2)]
        SAB = [view(o_sab + s * (PC + NSEQ * 19) * 4, F32, PC + NSEQ * 19) for s in range(2)]
        UFP = view(o_ufp, F32, 8 * 94).rearrange("p (j t) -> p j t", j=8)
        XBFP = view(o_xbfp, F32, 8 * 94).rearrange("p (j t) -> p j t", j=8)
        WOUT = [view(o_ht + s * 16 * 512 * 2, BF16, 16 * 512).rearrange("p (c e) -> p c e", c=16) for s in range(2)]
        WD = view(o_ht, BF16, 8 * D).rearrange("p (c e) -> p c e", c=8)
        UP = view(o_up, BF16, 8 * PC).rearrange("p (j t) -> p j t", j=8)
        US = view(o_us, BF16, 8 * NSEQ * 34).rearrange("p (j s t) -> p j s t", j=8, s=NSEQ)
        AT = view(o_r3b, BF16, 8 * NTOK).rearrange("p (c t) -> p c t", c=8)
        NXN2 = 6
        XN2 = [view(o_r3b + s * D * 2, BF16, D) for s in range(NXN2)]
        assert NXN2 * D * 2 <= 8 * PC * 2 + 8 * NSEQ * 34 * 2
        XBS = view(o_xbs, F32, 8 * NSEQ * 19).rearrange("p (j s t) -> p j s t", j=8, s=NSEQ)
        PTMP = [view(o_ptmp + s * NTOK * 2, BF16, NTOK) for s in range(2)]
        OUTST = [view(o_outst + s * CC * 4, F32, CC) for s in range(2)]
        SLD = [view(o_sld + s * CC * 4, F32, CC) for s in range(2)]
        SLDB = view(o_sldb, BF16, CC)
        PRMRAW = view(o_ufp, F32, CC)
        SLDX = [SLD[0], SLD[1], view(o_sig, F32, CC), view(o_sig + 4096, F32, CC),
                view(o_sab, F32, CC), view(o_sab + 4096, F32, CC)]
        NWU = 3
        WU = [view(o_r4 + s * 16 * 256 * 2, BF16, 16 * 256).rearrange("p (c e) -> p c e", c=16) for s in range(NWU)]
        RELU = [view(o_r3b + 8 * NTOK * 2 + s * 512 * 4, F32, 512) for s in range(3)]
        assert 8 * NTOK * 2 + 3 * 512 * 4 <= 8 * PC * 2 + 8 * NSEQ * 34 * 2
        DIAG = [view(o_diag + s * 31 * 128 * 2, BF16, 31 * 128).rearrange("p (k m) -> p k m", k=31) for s in range(2)]
        C32 = [view(o_ln + s * 2048, F32, 512) for s in range(2)] + [view(o_ptmp + s * 2048, F32, 512) for s in range(2)]
        CB = view(o_ln + 4096, BF16, 512)
        CSQ = view(o_ln + 5120, BF16, 512)
        M2 = view(o_ln + 6144, F32, 512)
        VV = view(o_sld, F32, 512)
        SG = view(o_sld + 2048, F32, 512)
        VVS = [VV, view(o_sld + 4096, F32, 512)]

        R = {}

        def res(name):
            if name not in R:
                R[name] = Res()
            return R[name]

        banks = [Res() for _ in range(8)]
        bank_ctr = [0]

        held = set()

        def next_bank(hold=False):
            while True:
                b = bank_ctr[0] % 8
                bank_ctr[0] += 1
                if b not in held:
                    break
            if hold:
                held.add(b)
            return b

        def pbank(b, parts=128, n=512):
            return psum[0:parts, b, 0:n]

        def pbank_bf(b, parts=128):
            return psum[0:parts, b, :].bitcast(BF16)

        dsems = {}

        def dsem(name):
            if name not in dsems:
                dsems[name] = DmaSem(nc, st, name)
            return dsems[name]

        r_identf, r_identb, r_onesd = res("identf"), res("identb"), res("onesd")
        r_stats = [[res("stat%d_%d" % (ph, t)) for t in range(9)] for ph in range(3)]
        r_stat_all = [r for l in r_stats for r in l]
        op(POOL, lambda: nc.gpsimd.memset(identf, 1.0), writes=[r_identf])
        op(POOL, lambda: nc.gpsimd.affine_select(out=identf, in_=identf, pattern=[[-1, 128]],
                                                 compare_op=ALU.is_ge, fill=0.0, base=0, channel_multiplier=1),
           reads=[r_identf], writes=[r_identf])
        op(POOL, lambda: nc.gpsimd.affine_select(out=identf, in_=identf, pattern=[[1, 128]],
                                                 compare_op=ALU.is_ge, fill=0.0, base=0, channel_multiplier=-1),
           reads=[r_identf], writes=[r_identf])
        op(POOL, lambda: nc.gpsimd.tensor_copy(identb, identf), reads=[r_identf], writes=[r_identb])
        op(POOL, lambda: nc.gpsimd.memset(onesd, 1.0 / 128.0), writes=[r_onesd])
        op(POOL, lambda: nc.gpsimd.memset(STAT, 0.0), writes=r_stat_all)
        op(POOL, lambda: nc.gpsimd.memset(EPSC, EPS), writes=[r_onesd])
        op(POOL, lambda: nc.gpsimd.memset(ONEC, 1.0), writes=[r_onesd])
        ACT.wait(r_onesd.w)

        r_gb = res("gb")

        def load_g(idx):
            dma(SP, dsem("gb"), GB, bass.AP(gs_t, idx * D, [[0, 128], [1, D]]), writes=[r_gb])

        r_prmraw, r_prm, r_wp = res("prmraw"), res("prm"), res("wp")
        r_xt = [res("xt%d" % i) for i in range(NXT)]
        r_xnr = [res("xn0"), res("xn1")]
        r_ht = res("ht")

        A0_ORDER = [0, 1, 2, 3, 4, 5, 6, 7, 8]

        def load_xt(pos):
            t = A0_ORDER[pos]
            nrows = 128 if t < 8 else 94
            nld = 128 if t < 8 else 96
            dma(SP, dsem("xt%d" % (pos % NXT)), XT[pos % NXT][0:nld, :], xin[128 * t:128 * t + nld, :],
                writes=[r_xt[pos % NXT]])
        load_xt(0)
        load_g(0)
        dma(SP, dsem("prmraw"), PRMRAW[0:35, :], prm_d, writes=[r_prmraw])
        for pos in range(1, NXT):
            load_xt(pos)
        r_sldx = [res("sldx%d" % i) for i in range(6)]
        for q in range(4):
            dma(SP, dsem("sldx%d" % q), SLDX[q][0:120, :], sc[4 * q:4 * q + 4].rearrange("s r c -> (s r) c"),
                writes=[r_sldx[q]])
        for q in range(2):
            dma(SP, dsem("sldx%d" % (4 + q)), SLDX[4 + q][0:120, :], sp[8 * q:8 * q + 8].rearrange("s r c -> (s r) c"),
                writes=[r_sldx[4 + q]])
        dma(POOL, dsem("wp"), WP, w_pool.rearrange("g (k p) d -> p g k d", p=128), writes=[r_wp])

        b = next_bank()

        def f_prm_tr():
            ins = None
            for j in range(8):
                ins = nc.tensor.transpose(psum[:, b, j * 35:(j + 1) * 35], PRMRAW[0:35, j * 128:(j + 1) * 128],
                                          identf[0:35, 0:35])
            return ins
        op(PE, f_prm_tr, reads=[r_prmraw, r_identf], writes=[banks[b]])
        op(ACT, lambda: nc.scalar.copy(PRM, psum[:, b, 0:280].rearrange("p (j k) -> p j k", j=8)),
           reads=[banks[b]], writes=[r_prm])
        op(DVE, lambda: nc.vector.tensor_scalar(out=NEGB, in0=PRM[:, :, 33], scalar1=-1.0, scalar2=None, op0=ALU.mult),
           reads=[r_prm], writes=[r_prm])

        r_sldb, r_xbs = res("sldb"), res("xbs")
        r_us = [res("us%d" % j) for j in range(8)]
        r_up = [res("up%d" % j) for j in range(8)]
        def conv_hist(q):
          if True:
            op(ACT, lambda: nc.scalar.copy(SLDB[0:120, :], SLDX[q][0:120, :]), reads=[r_sldx[q]], writes=[r_sldb])
            b = next_bank()
            bb = pbank_bf(b)

            def f_tr():
                ins = None
                for j in range(8):
                    ins = nc.tensor.transpose(bb[:, j * 120:(j + 1) * 120], SLDB[0:120, j * 128:(j + 1) * 128],
                                              identb[0:120, 0:120])
                return ins
            op(PE, f_tr, reads=[r_sldb, r_identb], writes=[banks[b]])
            op(ACT, lambda: nc.scalar.copy(US[:, :, 4 * q:4 * q + 4, 0:30],
                                           bb[:, 0:960].rearrange("p (j s r) -> p j s r", j=8, s=4)),
               reads=[banks[b]], writes=r_us)

        def pool_hist():
          for q in range(2):
            for hh in range(2):
                b = next_bank()

                def f_tr():
                    ins = None
                    for jj in range(4):
                        j = hh * 4 + jj
                        ins = nc.tensor.transpose(psum[:, b, jj * 120:(jj + 1) * 120],
                                                  SLDX[4 + q][0:120, j * 128:(j + 1) * 128], identf[0:120, 0:120])
                    return ins
                op(PE, f_tr, reads=[r_sldx[4 + q], r_identf], writes=[banks[b]])
                op(ACT, lambda: nc.scalar.copy(XBS[:, hh * 4:hh * 4 + 4, 8 * q:8 * q + 8, 0:15],
                                               psum[:, b, 0:480].rearrange("p (j s r) -> p j s r", j=4, s=8)),
                   reads=[banks[b]], writes=[r_xbs])

        def norm_a(t, src, nrows, phase, xn_slot, r_src, r_xn, part="both"):
            if part == "both":
                norm_a(t, src, nrows, phase, xn_slot, r_src, r_xn, "act")
                norm_a(t, src, nrows, phase, xn_slot, r_src, r_xn, "dve")
                return
            ss = STAT[0:nrows, 2 * phase, t:t + 1]
            rs = STAT[0:nrows, 2 * phase + 1, t:t + 1]
            r_stat = r_stats[phase][t]
            if part == "act":
                op(ACT, lambda: nc.scalar.activation(out=xn_slot[0:nrows, :], in_=src[0:nrows, :], func=AF.Square,
                                                     accum_out=ss), reads=[r_src], writes=[r_xn, r_stat])
                op(ACT, lambda: nc.scalar.activation(out=rs, in_=ss, func=AF.Ln, scale=1.0 / D, bias=EPSC[0:nrows, :]),
                   reads=[r_stat], writes=[r_stat])
                op(ACT, lambda: nc.scalar.activation(out=rs, in_=rs, func=AF.Exp, scale=-0.5),
                   reads=[r_stat], writes=[r_stat])
                return
            op(DVE, lambda: nc.vector.scalar_tensor_tensor(out=xn_slot[0:nrows, :], in0=src[0:nrows, :], scalar=rs,
                                                           in1=GB[0:nrows, :], op0=ALU.mult, op1=ALU.mult),
               reads=[r_src, r_stat, r_gb], writes=[r_xn])

        evac_ctr = [0]

        def norm_b(nrows, xn_slot, r_xn, evac):
            for half in range(2):
                b = next_bank()
                bb = pbank_bf(b)

                def f_tr():
                    ins = None
                    for c in range(8):
                        dc = half * 8 + c
                        ins = nc.tensor.transpose(bb[:, c * 128:c * 128 + nrows],
                                                  xn_slot[0:nrows, dc * 128:(dc + 1) * 128],
                                                  identb[0:nrows, 0:nrows])
                    return ins
                op(PE, f_tr, reads=[r_xn, r_identb], writes=[banks[b]])
                evac(b, bb, half, ACT if (half == 0 and evac_ctr[0] % 2 == 0) else DVE)
            evac_ctr[0] += 1

        def cp(eng, out, in_):
            return nc.scalar.copy(out, in_) if eng is ACT else nc.vector.tensor_copy(out, in_)

        def a0_a(pos):
            t = A0_ORDER[pos]
            nrows = 128 if t < 8 else 94
            norm_a(t, XT[pos % NXT], nrows, 0, XN[pos % 2], r_xt[pos % NXT], r_xnr[pos % 2])

        def a0_b(pos):
            t = A0_ORDER[pos]
            nrows = 128 if t < 8 else 94

            def evac(b, bb, half, eng):
                src3 = bb.rearrange("p (c t) -> p c t", c=8)
                if t < 8:
                    op(eng, lambda: cp(eng, HT[:, half * 8:half * 8 + 8, NH + 128 * t:NH + 128 * t + 128],
                                       src3[:, :, 0:128]), reads=[banks[b]], writes=[r_ht])
                else:
                    op(eng, lambda: cp(eng, HT[:, half * 8:half * 8 + 8, PC:PC + NS], src3[:, :, 0:NS]),
                       reads=[banks[b]], writes=[r_ht])
                    op(eng, lambda: cp(eng, HT[:, half * 8:half * 8 + 8, 0:NH], src3[:, :, NS:NS + NH]),
                       reads=[banks[b]], writes=[r_ht])
            norm_b(nrows, XN[pos % 2], r_xnr[pos % 2], evac)

        a0_a(0)
        for t in range(9):
            if t + 1 < 9:
                a0_a(t + 1)
            if t + NXT < 9:
                load_xt(t + NXT)
            a0_b(t)
        pool_hist()

        TB = [(0, 512), (512, 512), (1024, 94)]
        r_win = [res("win%d" % i) for i in range(4)]
        r_sig = [res("sig0"), res("sig1")]
        r_xbp = [res("xbp0"), res("xbp1")]
        r_sab = [res("sab0"), res("sab1")]
        r_ufp, r_xbfp = res("ufp"), res("xbfp")
        r_dt = [res("dt%d" % j) for j in range(8)]
        r_ct = [res("ct%d" % j) for j in range(8)]
        r_ptmp = [res("ptmp0"), res("ptmp1")]
        r_diag = [res("diag0"), res("diag1")]
        r_ln = {k: res("ln_" + k) for k in ("c32a", "c32b", "cb", "csq", "m2", "vv", "sg", "zz")}
        win_ctr = [0]

        def load_win(blk, slot=None):
            if slot is None:
                s_ = win_ctr[0] % 2
                win_ctr[0] += 1
            else:
                s_ = slot
            dma(POOL, dsem("win%d" % s_), WIN[s_],
                w_in[:, 256 * blk:256 * blk + 256].rearrange("(c p) e -> p c e", p=128), writes=[r_win[s_]])
            return s_

        def win_chunk(s_, hh, consumer):
            for tb, (c0, n) in enumerate(TB):
                b = next_bank()

                def f_mm():
                    ins = None
                    for dc in range(16):
                        ins = nc.tensor.matmul(pbank(b, 128, n), WIN[s_][:, dc, hh * 128:(hh + 1) * 128],
                                               HT[:, dc, c0:c0 + n], start=(dc == 0), stop=(dc == 15))
                    return ins
                op(PE, f_mm, reads=[r_win[s_], r_ht], writes=[banks[b]])
                consumer(tb, c0, n, b)

        def pooling(j, xs):
            w = WINDOWS[j // 2]
            steps = {2: 1, 4: 2, 8: 3, 16: 4}[w]
            X = XBP[xs]
            XS = XBS[:, j]
            srcP, srcS = X, XS
            r_src = r_xbp[xs]
            for i in range(1, steps + 1):
                sh = 2 ** (i - 1)
                lo = 2 ** i - 1
                d_ = (i - 1) % 2
                dstP = SAB[d_][:, 0:PC]
                dstS = SAB[d_][:, PC:PC + NSEQ * 19].rearrange("p (s t) -> p s t", s=NSEQ)
                rd = [r_src, r_xbs] if i == 1 else [r_src]
                op(DVE, lambda: nc.vector.tensor_tensor(out=dstP[:, lo:PC], in0=srcP[:, lo:PC], in1=srcP[:, lo - sh:PC - sh],
                                                        op=ALU.add), reads=rd, writes=[r_sab[d_]])
                op(DVE, lambda: nc.vector.tensor_tensor(out=dstS[:, :, lo:19], in0=srcS[:, :, lo:19],
                                                        in1=srcS[:, :, lo - sh:19 - sh], op=ALU.add),
                   reads=rd, writes=[r_sab[d_]])
                srcP, srcS, r_src = dstP, dstS, r_sab[d_]
            op(DVE, lambda: nc.vector.scalar_tensor_tensor(out=DT[:, j, 0:NPR], in0=srcP[:, NH:PC], scalar=1.0 / w,
                                                           in1=X[:, NH:PC], op0=ALU.mult, op1=ALU.subtract),
               reads=[r_src, r_xbp[xs]], writes=[r_dt[j]])
            op(DVE, lambda: nc.vector.scalar_tensor_tensor(
                out=DT[:, j, NPR:NTOK].rearrange("p (s t) -> p s t", s=NSEQ), in0=srcS[:, :, 15:19], scalar=1.0 / w,
                in1=XS[:, :, 15:19], op0=ALU.mult, op1=ALU.subtract),
               reads=[r_src, r_xbs, r_xbp[xs]], writes=[r_dt[j]])

        def pool_mm(gi):
            TBK = [(0, 512), (512, 512), (1024, 64)]
            for oc in range(2):
                jo = 2 * gi + oc
                for (c0, n) in TBK:
                    b = next_bank()

                    def f_mm():
                        ins = None
                        for kc in range(2):
                            ins = nc.tensor.matmul(pbank(b, 128, n), WP[:, gi, kc, oc * 128:(oc + 1) * 128],
                                                   DT[:, 2 * gi + kc, c0:c0 + n], start=(kc == 0), stop=(kc == 1))
                        return ins
                    op(PE, f_mm, reads=[r_wp, r_dt[2 * gi], r_dt[2 * gi + 1]], writes=[banks[b]])
                    if oc == 0:
                        op(ACT, lambda: nc.scalar.activation(out=PTMP[gi % 2][:, c0:c0 + n], in_=pbank(b, 128, n),
                                                             func=AF.Identity, scale=PRM[:, jo, 34:35]),
                           reads=[banks[b], r_prm], writes=[r_ptmp[gi % 2]])
                    else:
                        op(ACT, lambda: nc.scalar.activation(out=DT[:, jo, c0:c0 + n], in_=pbank(b, 128, n),
                                                             func=AF.Identity, scale=PRM[:, jo, 34:35]),
                           reads=[banks[b], r_prm], writes=[r_dt[jo]])
            op(DVE, lambda: nc.vector.tensor_copy(DT[:, 2 * gi, :], PTMP[gi % 2]), reads=[r_ptmp[gi % 2]],
               writes=[r_dt[2 * gi]])

        def build_diag(j):
            op(DVE, lambda: nc.vector.tensor_tensor(out=DIAG[j % 2], in0=bc(identb, [[0, 31], [1, 128]]),
                                                    in1=bc(PRM[:, j, 0:31], [[1, 31], [0, 128]]), op=ALU.mult),
               reads=[r_identb, r_prm], writes=[r_diag[j % 2]], free=[r_xt[0], r_xt[1], r_xnr[0], r_xnr[1]])

        def conv_chunk(j):
            ds_ = j % 2
            if j + 1 < 8:
                build_diag(j + 1)
            TBK = [(0, 512), (512, 512), (1024, 64)]
            units = []
            for tb, (c0, n) in enumerate(TBK):
                b = next_bank(hold=True)

                def f_mm():
                    ins = None
                    for k in range(31):
                        if tb < 2:
                            rhs = UP[:, j, c0 + k:c0 + k + n]
                            out = pbank(b, 128, n)
                        else:
                            rhs = US[:, j, :, k:k + 4]
                            out = pbank(b, 128, n).rearrange("p (s t) -> p s t", s=NSEQ)
                        ins = nc.tensor.matmul(out, DIAG[ds_][:, k, :], rhs, start=(k == 0), stop=(k == 30))
                    return ins
                op(PE, f_mm, reads=[r_diag[ds_], r_up[j], r_us[j]], writes=[banks[b]])
                ln_push((j, (tb, c0, n, b)))

        ln_ctr = [0]
        r_c32 = [res("c32_%d" % i) for i in range(4)]
        r_vv = [res("vv0"), res("vv1")]

        def ln_s1(item):
            j, tb, c0, n, b = item
            k = ln_ctr[0]
            ln_ctr[0] += 1
            c32 = C32[k % 4][:, 0:n]
            rc = r_c32[k % 4]
            bias = PRM[:, j, 31:32]
            fr = [r_ptmp[0], r_ptmp[1]] if k % 4 >= 2 else []
            op(ACT, lambda: nc.scalar.activation(out=c32, in_=pbank(b, 128, n), func=AF.Identity, bias=bias),
               reads=[banks[b], r_prm], writes=[rc], free=fr)
            op(DVE, lambda: nc.vector.tensor_copy(CB[:, 0:n], c32), reads=[rc], writes=[r_ln["cb"]])
            op(ACT, lambda: nc.scalar.activation(out=CSQ[:, 0:n], in_=pbank(b, 128, n), func=AF.Square, bias=bias),
               reads=[banks[b], r_prm], writes=[r_ln["csq"]])
            held.discard(b)
            bm, be = next_bank(hold=True), next_bank(hold=True)
            op(PE, lambda: nc.tensor.matmul(pbank(bm, 128, n), onesd, CB[:, 0:n], start=True, stop=True),
               reads=[r_onesd, r_ln["cb"]], writes=[banks[bm]])
            op(PE, lambda: nc.tensor.matmul(pbank(be, 128, n), onesd, CSQ[:, 0:n], start=True, stop=True),
               reads=[r_onesd, r_ln["csq"]], writes=[banks[be]])
            return (j, c0, n, k, bm, be)

        def ln_s2(item):
            j, c0, n, k, bm, be = item
            c32, rc = C32[k % 4][:, 0:n], r_c32[k % 4]
            vv, rv = VVS[k % 2][:, 0:n], r_vv[k % 2]
            op(ACT, lambda: nc.scalar.activation(out=M2[:, 0:n], in_=pbank(bm, 128, n), func=AF.Square),
               reads=[banks[bm]], writes=[r_ln["m2"]])
            op(DVE, lambda: nc.vector.tensor_tensor(out=vv, in0=pbank(be, 128, n), in1=M2[:, 0:n], op=ALU.subtract),
               reads=[banks[be], r_ln["m2"]], writes=[rv])
            op(DVE, lambda: nc.vector.tensor_tensor(out=c32, in0=c32, in1=pbank(bm, 128, n), op=ALU.subtract),
               reads=[rc, banks[bm]], writes=[rc])
            held.discard(bm)
            held.discard(be)
            return (j, c0, n, k)

        def ln_s3(item):
            j, c0, n, k = item
            c32, rc = C32[k % 4][:, 0:n], r_c32[k % 4]
            vv, rv = VVS[k % 2][:, 0:n], r_vv[k % 2]
            op(ACT, lambda: nc.scalar.activation(out=vv, in_=vv, func=AF.Ln, bias=EPSC), reads=[rv], writes=[rv])
            op(ACT, lambda: nc.scalar.activation(out=vv, in_=vv, func=AF.Exp, scale=-0.5), reads=[rv], writes=[rv])
            op(DVE, lambda: nc.vector.scalar_tensor_tensor(out=c32, in0=c32, scalar=PRM[:, j, 32:33], in1=vv,
                                                           op0=ALU.mult, op1=ALU.mult),
               reads=[rc, rv, r_prm], writes=[rc])
            return (j, c0, n, k)

        def ln_s4(item):
            j, c0, n, k = item
            c32, rc = C32[k % 4][:, 0:n], r_c32[k % 4]
            lb = PRM[:, j, 33:34]
            sg = SG[:, 0:n]
            op(ACT, lambda: nc.scalar.activation(out=sg, in_=c32, func=AF.Exp, scale=-1.0, bias=NEGB[:, j:j + 1]),
               reads=[rc, r_prm], writes=[r_ln["sg"]])
            op(ACT, lambda: nc.scalar.activation(out=sg, in_=sg, func=AF.Ln, bias=ONEC), reads=[r_ln["sg"]],
               writes=[r_ln["sg"]])
            op(ACT, lambda: nc.scalar.activation(out=sg, in_=sg, func=AF.Exp, scale=-1.0), reads=[r_ln["sg"]],
               writes=[r_ln["sg"]])
            op(DVE, lambda: nc.vector.scalar_tensor_tensor(out=CT[:, j, c0:c0 + n], in0=c32, scalar=lb, in1=SG[:, 0:n],
                                                           op0=ALU.add, op1=ALU.mult),
               reads=[rc, r_ln["sg"], r_prm], writes=[r_ct[j]])

        lq = [[], [], [], []]

        def ln_step(lag):
            if lq[3]:
                ln_s4(lq[3].pop(0))
            if lq[2]:
                lq[3].append(ln_s3(lq[2].pop(0)))
            if lq[1]:
                lq[2].append(ln_s2(lq[1].pop(0)))
            if len(lq[0]) > lag:
                lq[1].append(ln_s1(lq[0].pop(0)))

        def ln_push(item):
            j, (tb, c0, n, b) = item
            lq[0].append((j, tb, c0, n, b))
            ln_step(1)

        def ln_drain():
            while any(lq):
                ln_step(0)

        build_diag(0)
        xb_slots = [load_win(8), load_win(9)]
        POOL.wait(r_xt[NXT - 1].w)
        xb_slots += [load_win(10, slot=2), load_win(11, slot=3)]
        for blk in range(8, 12):
            s_ = xb_slots[blk - 8]
            for hh in range(2):
                j = (blk - 8) * 2 + hh
                xs = j % 2

                def cons_xb(tb, c0, n, b, j=j, xs=xs):
                    if tb < 2:
                        op(ACT, lambda: nc.scalar.copy(XBP[xs][:, c0:c0 + n], pbank(b, 128, n)),
                           reads=[banks[b]], writes=[r_xbp[xs]])
                    else:
                        op(ACT, lambda: nc.scalar.copy(XBFP[:, j, :], pbank(b, 128, 94)), reads=[banks[b]],
                           writes=[r_xbfp])
                        op(ACT, lambda: nc.scalar.copy(XBP[xs][:, 1024:PC], pbank(b, 128, 30)), reads=[banks[b]],
                           writes=[r_xbp[xs]])
                        op(ACT, lambda: nc.scalar.copy(XBS[:, j, :, 15:19],
                                                       psum[:, b, 30:94].rearrange("p (s t) -> p s t", s=NSEQ)),
                           reads=[banks[b]], writes=[r_xbs])
                win_chunk(s_, hh, cons_xb)
                if j < 4:
                    conv_hist(j)
                pooling(j, xs)
        r_x1 = [res("x1_%d" % t) for t in range(9)]
        r_wout = [res("wout0"), res("wout1")]
        TT = [(128 * t, 128) for t in range(8)] + [(1024, 64)]

        def load_wout(nb):
            s_ = nb % 2
            dma(POOL, dsem("wout%d" % s_), WOUT[s_],
                w_out[:, 512 * nb:512 * nb + 512].rearrange("(c p) e -> p c e", p=128),
                writes=[r_wout[s_]], free=[r_ht])

        def load_x1(t, free):
            r0, m_ = TT[t]
            dma(SP, dsem("x1_%d" % t), X1[t][0:m_, :], xin[r0:r0 + m_, :], writes=[r_x1[t]], free=free)

        conv_units = {}
        ln1 = {}

        def do_conv(jlist):
            for j in jlist:
                conv_chunk(j)

        for m in range(4):
            s_g = load_win(4 + m)
            for hh in range(2):
                j = 2 * m + hh

                def cons_gate(tb, c0, n, b, j=j):
                    sg = SIG[j % 2][:, c0:c0 + n]
                    op(ACT, lambda: nc.scalar.activation(out=sg, in_=pbank(b, 128, n), func=AF.Exp, scale=-1.0),
                       reads=[banks[b]], writes=[r_sig[j % 2]])
                    op(ACT, lambda: nc.scalar.activation(out=sg, in_=sg, func=AF.Ln, bias=ONEC), reads=[r_sig[j % 2]],
                       writes=[r_sig[j % 2]])
                    op(ACT, lambda: nc.scalar.activation(out=sg, in_=sg, func=AF.Exp, scale=-1.0), reads=[r_sig[j % 2]],
                       writes=[r_sig[j % 2]])
                win_chunk(s_g, hh, cons_gate)
            s_a = load_win(m)
            for hh in range(2):
                j = 2 * m + hh

                def cons_a(tb, c0, n, b, j=j):
                    if tb < 2:
                        op(DVE, lambda: nc.vector.tensor_tensor(out=UP[:, j, c0:c0 + n], in0=pbank(b, 128, n),
                                                                in1=SIG[j % 2][:, c0:c0 + n], op=ALU.mult),
                           reads=[banks[b], r_sig[j % 2]], writes=[r_up[j]])
                    else:
                        op(DVE, lambda: nc.vector.tensor_tensor(out=UFP[:, j, :], in0=pbank(b, 128, 94),
                                                                in1=SIG[j % 2][:, 1024:NCOL], op=ALU.mult),
                           reads=[banks[b], r_sig[j % 2]], writes=[r_ufp])
                        op(DVE, lambda: nc.vector.tensor_copy(UP[:, j, 1024:PC], UFP[:, j, 0:30]), reads=[r_ufp],
                           writes=[r_up[j]])
                        op(DVE, lambda: nc.vector.tensor_copy(US[:, j, :, 30:34],
                                                              UFP[:, j, 30:94].rearrange("p (s t) -> p s t", s=NSEQ)),
                           reads=[r_ufp], writes=[r_us[j]])
                win_chunk(s_a, hh, cons_a)
            if m == 0:
                pool_mm(0)
                pool_mm(1)
            if m == 1:
                pool_mm(2)
                pool_mm(3)
            if m == 3:
                load_wout(0)
                load_wout(1)
                load_x1(0, [r_win[0], r_win[1]])
                load_x1(1, [r_win[0], r_win[1]])
            if m >= 1:
                do_conv([2 * m - 1])
            do_conv([2 * m])

        def state_out(SRC, r_src, slot, prompt_rows, dst_p, dst_s, s_off, name):
            ob = [next_bank(), next_bank()]
            for hh in range(2):
                b = ob[hh]

                def f_tr():
                    ins = None
                    for jj in range(4):
                        ins = nc.tensor.transpose(psum[0:94, b, jj * 128:(jj + 1) * 128], SRC[:, hh * 4 + jj, :], identf)
                    return ins
                op(PE, f_tr, reads=[r_src, r_identf], writes=[banks[b]])
                op(ACT, lambda: nc.scalar.copy(OUTST[slot][0:94, hh * 512:(hh + 1) * 512], psum[0:94, b, :]),
                   reads=[banks[b]], writes=[res("outst%d" % slot)], free=[r_sig[0], r_sig[1]])
            r_o = res("outst%d" % slot)
            lo = 30 - prompt_rows
            dma(SP, dsem(name), dst_p, OUTST[slot][lo:30, :], reads=[r_o])
            for s in range(NSEQ):
                dma(SP, dsem(name), dst_s[s, s_off:s_off + 4, :], OUTST[slot][30 + 4 * s:34 + 4 * s, :], reads=[r_o])

        state_out(UFP, r_ufp, 0, 30, ncp, ncs, 26, "st_u")
        state_out(XBFP, r_xbfp, 1, 15, npp, nps, 11, "st_x")
        r3a_users = [r_sig[0], r_sig[1], r_xbp[0], r_xbp[1], r_sab[0], r_sab[1], r_ufp, r_xbfp,
                     res("outst0"), res("outst1"), r_prmraw] + r_sldx
        for t in range(2, 6):
            load_x1(t, r3a_users)
        do_conv([7])
        ln_step(0)
        NPRE = 5
        PRE_EC = [8 + i for i in range(8)] + [0, 1, 2, 3, 4, 5]
        pre_banks = {}

        def cp_chunk(ec):
            return (CT[:, ec], r_ct[ec]) if ec < 8 else (DT[:, ec - 8], r_dt[ec - 8])

        for t in range(NPRE):
            r0, m_ = TT[t]
            b = next_bank(hold=True)
            pre_banks[t] = b

            def f_mm():
                ins = None
                for i, ec in enumerate(PRE_EC):
                    a_, _ = cp_chunk(ec)
                    ins = nc.tensor.matmul(pbank(b, m_, 512), a_[:, r0:r0 + m_], WOUT[0][:, ec, :],
                                           start=(i == 0), stop=False)
                return ins
            op(PE, f_mm, reads=[r_wout[0]] + [r_ct[e] for e in range(6)] + r_dt, writes=[banks[b]])
        ln_drain()
        xtxn_users = [r_xt[0], r_xt[1], r_xnr[0], r_xnr[1], r_diag[0], r_diag[1]] + \
            [r_ln[k] for k in ("cb", "csq", "m2")] + [r_c32[0], r_c32[1]]
        for t in range(6, 9):
            load_x1(t, xtxn_users)
        dma(SP, dsem("hist"), ncs[:, 0:26, :], sc[:, 4:30, :])
        dma(SP, dsem("hist"), nps[:, 0:11, :], sp[:, 4:15, :])

        if debug:
            dma(SP, dsem("dbg"), dbg_ct, view(o_ct, BF16, 8 * NTOK), reads=r_ct)
            dma(SP, dsem("dbg"), dbg_dt, view(o_dt, BF16, 8 * NTOK), reads=r_dt)
        r_h2t = res("h2t")
        r_xn2 = [res("xn2_%d" % i) for i in range(NXN2)]
        r1_users = r_ct + r_dt
        r3b_users = r_up + r_us
        load_g(1)

        def d0_a(t, part="both"):
            r0, m_ = TT[t]
            norm_a(t, X1[t], m_, 1, XN2[t % NXN2], r_x1[t], r_xn2[t % NXN2], part)

        def d0_b(t):
            r0, m_ = TT[t]

            def evac(b, bb, half, eng):
                src3 = bb.rearrange("p (c t) -> p c t", c=8)
                op(eng, lambda: cp(eng, H2T[:, half * 8:half * 8 + 8, r0:r0 + m_], src3[:, :, 0:m_]),
                   reads=[banks[b]], writes=[r_h2t], free=r1_users)
            norm_b(m_, XN2[t % NXN2], r_xn2[t % NXN2], evac)

        for nb in range(4):
            s_ = nb % 2
            for t in range(9):
                r0, m_ = TT[t]
                pre = (nb == 0 and t in pre_banks)
                b = pre_banks[t] if pre else next_bank()
                ecs = [6, 7] if pre else list(range(16))

                def f_mm():
                    ins = None
                    for i, ec in enumerate(ecs):
                        a_, _ = cp_chunk(ec)
                        ins = nc.tensor.matmul(pbank(b, m_, 512), a_[:, r0:r0 + m_], WOUT[s_][:, ec, :],
                                               start=(i == 0 and not pre), stop=(i == len(ecs) - 1))
                    return ins
                op(PE, f_mm, reads=[r_wout[s_]] + r_ct + r_dt, writes=[banks[b]])
                xs_ = X1[t][0:m_, nb * 512:(nb + 1) * 512]
                op(DVE, lambda: nc.vector.tensor_tensor(out=xs_, in0=pbank(b, m_, 512), in1=xs_, op=ALU.add),
                   reads=[banks[b]], writes=[r_x1[t]])
                held.discard(b)
                if nb == 3 and t < NXN2:
                    if t == 0:
                        for rr in r3b_users:
                            DVE.wait(rr.w, *rr.r.values())
                            ACT.wait(rr.w, *rr.r.values())
                    d0_a(t, "act")
                if nb == 3 and 1 <= t <= NXN2:
                    d0_a(t - 1, "dve")
            if nb + 2 < 4:
                load_wout(nb + 2)

        if debug:
            for t in range(9):
                dma(SP, dsem("dbg"), dbg_x1[t], X1[t], reads=[r_x1[t]])
        for t in range(9):
            d0_b(t)
            if t + NXN2 < 9:
                d0_a(t + NXN2)

        r_wu = [res("wu%d" % i) for i in range(NWU)]
        r_wd = res("wd")
        r_at = [res("at%d" % i) for i in range(8)]
        r_relu = [res("relu%d" % i) for i in range(3)]
        r4_users = [r_xbs, r_ptmp[0], r_ptmp[1], r_sldx[0], r_sldx[1], r_sldb,
                    r_vv[0], r_vv[1], r_ln["sg"], r_c32[2], r_c32[3]]
        wu_ctr = [0]
        relu_ctr = [0]
        TBM = [(0, 363), (363, 363), (726, 362)]

        def load_wu(blk):
            s_ = blk % NWU
            dma(POOL, dsem("wu%d" % s_), WU[s_],
                w_up[:, 256 * blk:256 * blk + 256].rearrange("(c p) e -> p c e", p=128),
                writes=[r_wu[s_]], free=r4_users)

        def load_wd(g):
            dma(POOL, dsem("wd"), WD, w_dn[1024 * g:1024 * g + 1024, :].rearrange("(c p) e -> p c e", p=128),
                writes=[r_wd], free=[r_wout[0], r_wout[1], r_ht])

        for blk in range(NWU):
            load_wu(blk)
        load_g(2)
        def final_scale_store(t):
            r0, m_ = TT[t]
            rs = STAT[0:m_, 5, t:t + 1]
            op(DVE, lambda: nc.vector.scalar_tensor_tensor(out=X1[t][0:m_, :], in0=X1[t][0:m_, :], scalar=rs,
                                                           in1=GB[0:m_, :], op0=ALU.mult, op1=ALU.mult),
               reads=[r_stats[2][t], r_gb], writes=[r_x1[t]])
            dma(SP, dsem("yout"), y[r0:r0 + m_, :], X1[t][0:m_, :], reads=[r_x1[t]])

        for g in range(8):
            for bi in range(4):
                blk = 4 * g + bi
                s_ = blk % NWU
                for hh in range(2):
                    fcl = bi * 2 + hh
                    for (c0, n) in TBM:
                        b = next_bank()

                        def f_mm():
                            ins = None
                            for dc in range(16):
                                ins = nc.tensor.matmul(pbank(b, 128, n), WU[s_][:, dc, hh * 128:(hh + 1) * 128],
                                                       H2T[:, dc, c0:c0 + n], start=(dc == 0), stop=(dc == 15))
                            return ins
                        op(PE, f_mm, reads=[r_wu[s_], r_h2t], writes=[banks[b]])
                        rs_ = relu_ctr[0] % 3
                        relu_ctr[0] += 1
                        op(ACT, lambda: nc.scalar.activation(out=RELU[rs_][:, 0:n], in_=pbank(b, 128, n), func=AF.Relu),
                           reads=[banks[b]], writes=[r_relu[rs_]], free=r3b_users + r_xn2)
                        op(DVE, lambda: nc.vector.tensor_tensor(out=AT[:, fcl, c0:c0 + n], in0=RELU[rs_][:, 0:n],
                                                                in1=RELU[rs_][:, 0:n], op=ALU.mult),
                           reads=[r_relu[rs_]], writes=[r_at[fcl]], free=r3b_users + r_xn2)
                if bi == 0:
                    load_wd(g)
                if blk + NWU < 32:
                    load_wu(blk + NWU)
            for t in range(9):
                r0, m_ = TT[t]
                for nb in range(4):
                    b = next_bank()

                    def f_mm():
                        ins = None
                        for fc in range(8):
                            PE.wait(r_at[fc].w)
                            ins = nc.tensor.matmul(pbank(b, m_, 512), AT[:, fc, r0:r0 + m_],
                                                   WD[:, fc, nb * 512:(nb + 1) * 512], start=(fc == 0), stop=(fc == 7))
                        return ins
                    tk = op(PE, f_mm, reads=[r_wd], writes=[banks[b]])
                    for fc in range(8):
                        r_at[fc].r[tk[0]] = tk
                    xs_ = X1[t][0:m_, nb * 512:(nb + 1) * 512]
                    op(DVE, lambda: nc.vector.tensor_tensor(out=xs_, in0=pbank(b, m_, 512), in1=xs_, op=ALU.add),
                       reads=[banks[b]], writes=[r_x1[t]])
                if g == 7:
                    ss = STAT[0:m_, 4, t:t + 1]
                    rs = STAT[0:m_, 5, t:t + 1]
                    r_stat = r_stats[2][t]
                    op(ACT, lambda: nc.scalar.activation(out=JUNK[0:m_, :], in_=X1[t][0:m_, :], func=AF.Square,
                                                         accum_out=ss),
                       reads=[r_x1[t]], writes=[r_h2t, r_stat])
                    op(ACT, lambda: nc.scalar.activation(out=rs, in_=ss, func=AF.Ln, scale=1.0 / D,
                                                         bias=EPSC[0:m_, :]), reads=[r_stat], writes=[r_stat])
                    op(ACT, lambda: nc.scalar.activation(out=rs, in_=rs, func=AF.Exp, scale=-0.5),
                       reads=[r_stat], writes=[r_stat])
                    if t >= 1:
                        final_scale_store(t - 1)
        final_scale_store(8)

        for name in ("yout", "st_u", "st_x", "hist") + (("dbg",) if debug else ()):
            d_ = dsems[name]
            SP.h.wait_ge(d_.sem, d_.n)
    return nc


_NC_CACHE = {}


def kernel(x_prompt, x_sample, state_conv, state_pool, meta_tokens, norm_mix_g, w_in, w_dw, b_dw,
           conv_ln_g, conv_ln_b, w_pool, pool_scale, w_out, norm_ffn_g, w_up, w_down, final_norm_g):
    f = lambda a: np.ascontiguousarray(np.asarray(a, dtype=np.float32))
    x_prompt, x_sample, state_conv, state_pool, meta_tokens = map(f, (x_prompt, x_sample, state_conv, state_pool, meta_tokens))
    gs = f(np.stack([np.asarray(norm_mix_g)[0], np.asarray(norm_ffn_g)[0], np.asarray(final_norm_g)], axis=0))
    prm = f(np.concatenate([np.asarray(w_dw)[0], np.asarray(b_dw), np.asarray(conv_ln_g), np.asarray(conv_ln_b),
                            np.asarray(pool_scale)], axis=0))
    w_in_, w_pool_, w_out_, w_up_, w_down_ = f(w_in)[0], f(w_pool)[0], f(w_out)[0], f(w_up)[0], f(w_down)[0]
    in_maps = []
    for i in range(8):
        b, h = i // 2, i % 2
        if h == 0:
            halo = np.concatenate([np.zeros((NH - 16, D), np.float32), meta_tokens], axis=0)
        else:
            halo = x_prompt[b, NPR - NH:NPR]
        xin = np.concatenate([x_prompt[b, h * NPR:(h + 1) * NPR], x_sample[16 * i:16 * i + 16].reshape(NS, D), halo,
                              np.zeros((2, D), np.float32)], axis=0)
        in_maps.append({
            "xin": np.ascontiguousarray(xin), "sc": np.ascontiguousarray(state_conv[0, 16 * i:16 * i + 16]),
            "sp": np.ascontiguousarray(state_pool[0, 16 * i:16 * i + 16]), "gs": gs, "prm": prm,
            "w_in": w_in_, "w_pool": w_pool_, "w_out": w_out_, "w_up": w_up_, "w_down": w_down_,
        })
    if "nc" not in _NC_CACHE:
        _NC_CACHE["nc"] = build()
    res = run_bass_kernel_spmd(_NC_CACHE["nc"], in_maps, core_ids=list(range(8)))
    outs = res.results
    y_prompt = np.zeros((4, 2048, D), np.float32)
    y_sample = np.zeros((128, 4, D), np.float32)
    ncp = np.zeros((1, 4, 30, CC), np.float32)
    npp = np.zeros((1, 4, 15, CC), np.float32)
    ncs = np.zeros((1, 128, 30, CC), np.float32)
    nps = np.zeros((1, 128, 15, CC), np.float32)
    for i in range(8):
        b, h = i // 2, i % 2
        o = outs[i]
        y_prompt[b, h * NPR:(h + 1) * NPR] = o["y"][0:NPR]
        y_sample[16 * i:16 * i + 16] = o["y"][NPR:NTOK].reshape(16, 4, D)
        ncs[0, 16 * i:16 * i + 16] = o["ncs"]
        nps[0, 16 * i:16 * i + 16] = o["nps"]
        if h == 1:
            ncp[0, b] = o["ncp"]
            npp[0, b] = o["npp"]
    return (y_prompt, y_sample, ncp, npp, ncs, nps)
```
